# Optimizing a Trainium2 kernel written in Bass

```python
import math
import jax, jax.numpy as jnp
from jax import lax
import numpy as np

D_MODEL = 2048
BATCH = 32
SEQ = 256
DEPTH = 4
DEC_BATCH = 8
DEC_SEQ = 2048
PAST_LEN = 256

GRID_W = 64
N_HEADS = 16
KV_HEADS = 4
HEAD_DIM = 128
D_ATTN = N_HEADS * HEAD_DIM
D_KV = KV_HEADS * HEAD_DIM
D_CONV = D_MODEL // 2
CONV_W = 3
D_FF = -(-8 * D_MODEL // (3 * 256)) * 256
Q_BLOCK = 128
ROPE_THETA = 10000.0
EPS = 1e-6
IN_COLS = D_ATTN + 2 * D_KV + 3 * D_CONV + 2 * D_MODEL

kernel_name = "hybrid_conv_gqa_prefix_dit_step"


def rms_norm(x, gain):
    xf = x.astype(jnp.float32)
    y = xf * lax.rsqrt(jnp.mean(xf * xf, axis=-1, keepdims=True) + EPS)
    return (y * gain.astype(jnp.float32)).astype(x.dtype)


def adaln(cond, w_ada, b_ada):
    m = jnp.einsum('...d,de->...e', jax.nn.silu(cond), w_ada) + b_ada
    return jnp.split(m[..., None, :], 6, axis=-1)


def axial_rope(n_tokens):
    rows = n_tokens // GRID_W
    row = jnp.repeat(jnp.arange(rows, dtype=jnp.float32), GRID_W)
    col = jnp.tile(jnp.arange(GRID_W, dtype=jnp.float32), rows)
    n_freq = HEAD_DIM // 4
    inv = ROPE_THETA ** (-jnp.arange(n_freq, dtype=jnp.float32) / n_freq)
    ang = jnp.concatenate([row[:, None] * inv, col[:, None] * inv], axis=-1)
    return jnp.cos(ang), jnp.sin(ang)


def apply_rope(x, cos, sin):
    xf = x.astype(jnp.float32)
    x1, x2 = jnp.split(xf, 2, axis=-1)
    c = cos[None, :, None, :]
    s = sin[None, :, None, :]
    return jnp.concatenate([x1 * c - x2 * s, x2 * c + x1 * s], axis=-1).astype(x.dtype)


def blocked_attention(q, k, v):
    B, S, H, hd = q.shape
    G = H // KV_HEADS
    nb = S // Q_BLOCK
    qb = q.reshape(B, nb, Q_BLOCK, KV_HEADS, G, hd).transpose(1, 0, 2, 3, 4, 5)
    kf = k.astype(jnp.float32)
    vf = v.astype(jnp.float32)
    scale = 1.0 / math.sqrt(hd)

    def one_block(qblk):
        s = jnp.einsum('bqkgd,bskd->bkgqs', qblk.astype(jnp.float32) * scale, kf)
        p = jax.nn.softmax(s, axis=-1)
        return jnp.einsum('bkgqs,bskd->bqkgd', p, vf).astype(q.dtype)

    o = lax.map(one_block, qb)
    return o.transpose(1, 0, 2, 3, 4, 5).reshape(B, S, H * hd)


def short_conv(xc, w, b):
    xp = jnp.pad(xc, ((0, 0), (1, 1), (0, 0)))
    return xp[:, :-2] * w[0] + xp[:, 1:-1] * w[1] + xp[:, 2:] * w[2] + b


def trunk_layer(x, cond, ctx_kv, rope, p):
    (w_ada, b_ada, g1, w_in, q_gain, k_gain, conv_w, conv_b,
     w_a, w_b, w_o, g2, w_gate, w_up, w_down) = p
    B, S, _ = x.shape
    sh1, sc1, gt1, sh2, sc2, gt2 = adaln(cond, w_ada, b_ada)

    h = rms_norm(x, g1) * (1 + sc1) + sh1
    proj = h @ w_in
    cuts = [D_ATTN, D_ATTN + D_KV, D_ATTN + 2 * D_KV,
            D_ATTN + 2 * D_KV + D_CONV, D_ATTN + 2 * D_KV + 2 * D_CONV,
            D_ATTN + 2 * D_KV + 3 * D_CONV, D_ATTN + 2 * D_KV + 3 * D_CONV + D_MODEL]
    q, k, v, cv_b, cv_c, cv_x, gate_a, gate_b = jnp.split(proj, cuts, axis=-1)

    q = rms_norm(q.reshape(B, S, N_HEADS, HEAD_DIM), q_gain)
    k = rms_norm(k.reshape(B, S, KV_HEADS, HEAD_DIM), k_gain)
    v = v.reshape(B, S, KV_HEADS, HEAD_DIM)
    if rope is None:
        keys, vals = k, v
    else:
        q = apply_rope(q, *rope)
        k = apply_rope(k, *rope)
        keys = jnp.concatenate([k, ctx_kv[0]], axis=1)
        vals = jnp.concatenate([v, ctx_kv[1]], axis=1)
    o_attn = blocked_attention(q, keys, vals)

    y_conv = cv_b * short_conv(cv_c * cv_x, conv_w, conv_b)

    merged = jax.nn.sigmoid(gate_a) * (y_conv @ w_a) + jax.nn.sigmoid(gate_b) * (o_attn @ w_b)
    x = x + gt1 * (merged @ w_o)

    h2 = rms_norm(x, g2) * (1 + sc2) + sh2
    x = x + gt2 * ((jax.nn.silu(h2 @ w_gate) * (h2 @ w_up)) @ w_down)
    return x, (k, v)


def setup_inputs(seed: int = 0) -> dict:
    key = jax.random.key(seed)
    ks = jax.random.split(key, 24)
    f32 = jnp.float32

    def nrm(k, shape, scale=1.0):
        return jax.random.normal(k, shape, f32) * scale

    kv_shape = (DEC_BATCH, DEPTH, PAST_LEN, KV_HEADS, HEAD_DIM)
    return {
        "x_prompt": nrm(ks[0], (BATCH, SEQ, D_MODEL)),
        "x_sample": nrm(ks[1], (DEC_BATCH, DEC_SEQ, D_MODEL)),
        "cache_k": nrm(ks[2], kv_shape),
        "cache_v": nrm(ks[3], kv_shape),
        "c": nrm(ks[4], (DEC_BATCH, D_MODEL)),
        "c_ctx": nrm(ks[5], (D_MODEL,)),
        "w_ada": nrm(ks[6], (DEPTH, D_MODEL, 6 * D_MODEL), 0.5 * D_MODEL ** -0.5),
        "b_ada": nrm(ks[7], (DEPTH, 6 * D_MODEL), 0.02),
        "norm1": 1.0 + nrm(ks[8], (DEPTH, D_MODEL), 0.02),
        "w_in": nrm(ks[9], (DEPTH, D_MODEL, IN_COLS), D_MODEL ** -0.5),
        "q_gain": 1.0 + nrm(ks[10], (DEPTH, HEAD_DIM), 0.02),
        "k_gain": 1.0 + nrm(ks[11], (DEPTH, HEAD_DIM), 0.02),
        "conv_w": nrm(ks[12], (DEPTH, CONV_W, D_CONV), CONV_W ** -0.5),
        "conv_b": nrm(ks[13], (DEPTH, D_CONV), 0.02),
        "w_a": nrm(ks[14], (DEPTH, D_CONV, D_MODEL), D_CONV ** -0.5),
        "w_b": nrm(ks[15], (DEPTH, D_ATTN, D_MODEL), D_ATTN ** -0.5),
        "w_o": nrm(ks[16], (DEPTH, D_MODEL, D_MODEL), D_MODEL ** -0.5),
        "norm2": 1.0 + nrm(ks[17], (DEPTH, D_MODEL), 0.02),
        "w_gate": nrm(ks[18], (DEPTH, D_MODEL, D_FF), D_MODEL ** -0.5),
        "w_up": nrm(ks[19], (DEPTH, D_MODEL, D_FF), D_MODEL ** -0.5),
        "w_down": nrm(ks[20], (DEPTH, D_FF, D_MODEL), D_FF ** -0.5),
        "norm_f": 1.0 + nrm(ks[21], (D_MODEL,), 0.02),
    }


def reference(x_prompt, x_sample, cache_k, cache_v, c, c_ctx,
              w_ada, b_ada, norm1, w_in, q_gain, k_gain, conv_w, conv_b,
              w_a, w_b, w_o, norm2, w_gate, w_up, w_down, norm_f):
    rope = axial_rope(x_sample.shape[1])
    ctx = x_prompt
    lat = x_sample
    new_k = []
    new_v = []
    for l in range(DEPTH):
        p = (w_ada[l], b_ada[l], norm1[l], w_in[l], q_gain[l], k_gain[l],
             conv_w[l], conv_b[l], w_a[l], w_b[l], w_o[l], norm2[l],
             w_gate[l], w_up[l], w_down[l])
        ctx, (k_ctx, v_ctx) = trunk_layer(ctx, c_ctx, None, None, p)
        new_k.append(k_ctx)
        new_v.append(v_ctx)
        lat, _ = trunk_layer(lat, c, (cache_k[:, l], cache_v[:, l]), rope, p)
    y_prompt = rms_norm(ctx, norm_f)
    y_sample = rms_norm(lat, norm_f)
    new_cache_k = jnp.stack(new_k, axis=1)
    new_cache_v = jnp.stack(new_v, axis=1)
    return (y_prompt, y_sample, new_cache_k, new_cache_v)
```

```python
import math
from contextlib import ExitStack

import numpy as np
import concourse.bass as bass
import concourse.mybir as mybir
from concourse.bass_utils import run_bass_kernel_spmd

F32 = mybir.dt.float32
BF16 = mybir.dt.bfloat16
AF = mybir.ActivationFunctionType
ALU = mybir.AluOpType

D = 2048
DEPTH = 4
NCORE = 8
PSEQ = 256
NPSEQ = 4
STOK = 2048
PAST = 256
NTOK = NPSEQ * PSEQ + STOK
NT = NTOK // 512
KVH = 4
NH = 16
HD = 128
D_CONV = 1024
D_FF = 5632
NF = D_FF // 128
IN_COLS = 10240
EPS = 1e-6
C_Q, C_K, C_V, C_CB, C_CC, C_CX, C_GA, C_GB = 0, 2048, 2560, 3072, 4096, 5120, 6144, 8192
KTOK = NTOK + PAST

DBG = {}
ENGS = ("pe", "act", "dve", "pool", "sp")
NDMA = 12


class Plan:
    def __init__(self):
        self.ops = {e: [] for e in ENGS}
        self.count = {e: 0 for e in ENGS}
        self.waited = {e: {} for e in ENGS}
        self.last_w = {}
        self.reads = {}
        self.dq = {q: {"n": 0, "vals": [0] * NDMA} for q in ("sp", "pool")}

    def op(self, eng, fn, reads=(), writes=(), dma=False):
        deps = {}

        def add(tok):
            if tok is None:
                return
            k, v, e = tok
            if e == "pe" and eng == "pe":
                return
            if deps.get(k, 0) < v:
                deps[k] = v

        lw = self.last_w
        xr = [p for p in reads if isinstance(p, tuple) and p[0] == "ps"]
        if xr:
            reads = [p for p in reads if not (isinstance(p, tuple) and p[0] == "ps")]
            writes = list(writes) + [p for p in xr if p not in writes]
        for p in reads:
            add(lw.get(p))
        for p in writes:
            add(lw.get(p))
            rd = self.reads.get(p)
            if rd:
                for k, (v, e) in rd.items():
                    add((k, v, e))
        if dma:
            pool = self.dq[eng]
            idx = pool["n"] % NDMA
            pool["n"] += 1
            prev = pool["vals"][idx]
            if prev > 0:
                add((("dma", eng, idx), prev, None))
            pool["vals"][idx] = prev + 16
            mykey = ("dma", eng, idx)
            myval = prev + 16
            tok_e = None
        else:
            self.count[eng] += 1
            mykey = ("eng", eng)
            myval = self.count[eng]
            tok_e = eng
        waits = []
        wd = self.waited[eng]
        for k, v in deps.items():
            if wd.get(k, 0) >= v:
                continue
            wd[k] = v
            waits.append((k, v))
        self.ops[eng].append((waits, fn, mykey, dma))
        tok = (mykey, myval, tok_e)
        for p in writes:
            lw[p] = tok
            self.reads[p] = {}
        for p in reads:
            d = self.reads.get(p)
            if d is None:
                d = self.reads[p] = {}
            d[mykey] = (myval, tok_e)
        return tok

    def final_waits(self, eng):
        waits = []
        for e in ENGS:
            if self.count[e] > 0 and e != eng:
                waits.append((("eng", e), self.count[e]))
        for q, pool in self.dq.items():
            for idx, v in enumerate(pool["vals"]):
                if v > 0:
                    waits.append((("dma", q, idx), v))
        return waits


def all_semkeys():
    ks = [("eng", e) for e in ENGS]
    for q in ("sp", "pool"):
        for i in range(NDMA):
            ks.append(("dma", q, i))
    return ks


def emit(nc, plan, sems):
    fin = plan.final_waits("sp")
    with nc.Block() as block:
        def run(engname, e):
            for waits, fn, mykey, dma in plan.ops[engname]:
                for k, v in waits:
                    e.wait_ge(sems[k], v)
                ins = fn(e)
                ins.then_inc(sems[mykey], 16 if dma else 1)

        @block.tensor
        def _(e):
            run("pe", e)

        @block.scalar
        def _(e):
            run("act", e)

        @block.vector
        def _(e):
            run("dve", e)

        @block.gpsimd
        def _(e):
            run("pool", e)

        @block.sync
        def _(e):
            run("sp", e)
            for k, v in fin:
                e.wait_ge(sems[k], v)


R1P, R2P, WSP = 96, 52, 52
WRING = 24
STB = 24


def build(depth=DEPTH, maxph=None):
    nc = bass.Bass("TRN2", target_bir_lowering=False)

    def din(name, shape):
        return nc.dram_tensor(name, list(shape), F32, kind="ExternalInput")

    xin = din("xin", [NTOK, D])
    ck_d = din("ck", [DEPTH, PAST, 512])
    cv_d = din("cv", [DEPTH, PAST, 512])
    condT_d = din("condT", [128, 32])
    w_ada_d = din("w_ada", [depth, D, 6 * D])
    badaT_d = din("badaT", [DEPTH, 128, 96])
    g1T_d = din("g1T", [DEPTH, 128, 16])
    g2T_d = din("g2T", [DEPTH, 128, 16])
    gfT_d = din("gfT", [128, 16])
    qkg_d = din("qkg", [128, 2 * DEPTH])
    convT_d = din("convT", [DEPTH, 128, 32])
    w_in_d = din("w_in", [depth, D, IN_COLS])
    w_a_d = din("w_a", [depth, D_CONV, D])
    w_b_d = din("w_b", [depth, D, D])
    w_o_d = din("w_o", [depth, D, D])
    w_gate_d = din("w_gate", [depth, D, D_FF])
    w_up_d = din("w_up", [depth, D, D_FF])
    w_down_d = din("w_down", [depth, D_FF, D])
    ropeC_d = din("ropeC", [128, STOK])
    ropeS_d = din("ropeS", [128, STOK])
    ident_d = din("ident", [128, 128])
    rmT_d = din("rmT", [128, 128])

    yout = nc.dram_tensor("yout", [NTOK, D], F32, kind="ExternalOutput")
    nk_d = nc.dram_tensor("nk", [NPSEQ, DEPTH, PSEQ, 512], F32, kind="ExternalOutput")
    nv_d = nc.dram_tensor("nv", [NPSEQ, DEPTH, PSEQ, 512], F32, kind="ExternalOutput")

    xT = nc.dram_tensor("xT_s", [NT, 128, 16, 512], F32)
    Osp = nc.dram_tensor("O_s", [NT, 128, 16, 512], BF16)
    Gsp = nc.dram_tensor("G_s", [NT, 128, 32, 512], BF16)
    Msp = nc.dram_tensor("M_s", [NT, 128, 16, 512], BF16)
    Asp = nc.dram_tensor("A_s", [NT, 128, NF, 512], BF16)

    es = ExitStack()
    with es:
        def sb(name, shape, dt):
            return es.enter_context(nc.sbuf_tensor("sb_" + name, list(shape), dt))

        R1 = sb("R1", [128, R1P * 512], BF16)
        R2 = sb("R2", [128, R2P * 512], BF16)
        WS = sb("WS", [128, WSP * 512], BF16)
        ARENA = {"R1": R1, "R2": R2, "WS": WS}

        ones16 = sb("ones16", [128, 128], BF16)
        id32 = sb("id32", [128, 128], F32)
        id16 = sb("id16", [128, 128], BF16)
        rm16 = sb("rm16", [128, 128], BF16)
        condS = sb("condS", [128, 32], F32)
        scT = sb("scT", [128, 32], BF16)
        badaT = sb("badaT", [128, 96], F32)
        modT = sb("modT", [128, 192], F32)
        g1T = sb("g1T", [128, 16], F32)
        g2T = sb("g2T", [128, 16], F32)
        gfT = sb("gfT", [128, 16], F32)
        AB = sb("AB", [128, 4 * 2 * 16], F32)
        GT = sb("GT", [128, 2 * 2 * 2 * 16], F32)
        qkg = sb("qkg", [128, 2 * DEPTH], F32)
        convT = sb("convT", [128, 32], F32)

        banks = [es.enter_context(nc.psum_tensor(f"ps{i}", [128, 512], F32)) for i in range(8)]
        sems = {k: es.enter_context(nc.semaphore("s_" + "_".join(str(x) for x in k))) for k in all_semkeys()}

        P = Plan()

        def pg(arena, first, n=1):
            return [(arena, p) for p in range(first, first + n)]

        def bfp(arena, page, n=1):
            return ARENA[arena][:, page * 512:(page + n) * 512]

        def f32p(arena, page, n=2):
            return ARENA[arena][:, page * 512:(page + n) * 512].bitcast(F32)

        class PsumAlloc:
            def __init__(self):
                self.free = list(range(8))

            def alloc(self):
                assert self.free, "psum exhausted"
                return self.free.pop(0)

            def release(self, b):
                self.free.append(b)

        psa = PsumAlloc()

        def PSK(b):
            return [("ps", b)]

        wstate = {"pos": 0, "limit": WRING}

        def walloc(n):
            if wstate["pos"] + n > wstate["limit"]:
                wstate["pos"] = 0
            p = wstate["pos"]
            wstate["pos"] += n
            return p

        def dma(eng, out, in_, reads, writes):
            P.op(eng, lambda e: e.dma_start(out=out, in_=in_), reads=reads, writes=writes, dma=True)

        def wload(src2d, kc, ncols, page=None):
            npages = (kc * ncols + 511) // 512
            if page is None:
                page = walloc(npages)
            dst = WS[:, page * 512: page * 512 + kc * ncols].rearrange("p (k c) -> p k c", k=kc)
            keys = pg("WS", page, npages)
            dma("pool", dst, src2d.rearrange("(k p) c -> p k c", p=128), [], keys)
            return dst, keys

        def mm_group(bank_ap, pairs, reads, bank_keys, skip=False, first=True, last=True):
            n = len(pairs)

            def fn(e):
                ins = None
                for i, (l, r) in enumerate(pairs):
                    ins = e.matmul(bank_ap, l, r, start=(first and i == 0), stop=(last and i == n - 1))
                return ins
            P.op("pe", fn, reads=reads, writes=bank_keys)

        def act(out, in_, func, reads, writes, bias=0.0, scale=1.0):
            P.op("act", lambda e: e.activation(out=out, in_=in_, func=func, bias=bias, scale=scale), reads=reads, writes=writes)

        def dve_tt(out, in0, in1, op, reads, writes):
            P.op("dve", lambda e: e.tensor_tensor(out=out, in0=in0, in1=in1, op=op), reads=reads, writes=writes)

        def dve_stt(out, in0, scalar, in1, op0, op1, reads, writes):
            P.op("dve", lambda e: e.scalar_tensor_tensor(out=out, in0=in0, scalar=scalar, in1=in1, op0=op0, op1=op1), reads=reads, writes=writes)

        def dve_ts(out, in0, s1, s2, op0, op1, reads, writes):
            if s2 is None:
                P.op("dve", lambda e: e.tensor_scalar(out=out, in0=in0, scalar1=s1, scalar2=None, op0=op0), reads=reads, writes=writes)
            else:
                P.op("dve", lambda e: e.tensor_scalar(out=out, in0=in0, scalar1=s1, scalar2=s2, op0=op0, op1=op1), reads=reads, writes=writes)

        def dve_copy(out, in_, reads, writes):
            P.op("dve", lambda e: e.tensor_copy(out=out, in_=in_), reads=reads, writes=writes)

        def dve_recip(out, in_, reads, writes):
            P.op("dve", lambda e: e.reciprocal(out=out, in_=in_), reads=reads, writes=writes)

        def dve_recip2(x, scr, reads, writes):
            P.op("dve", lambda e: e.reciprocal_approx_accurate(out=x, in_=x, scratch=scr), reads=reads, writes=writes)

        def pool_tt(out, in0, in1, op, reads, writes):
            P.op("pool", lambda e: e.tensor_tensor(out=out, in0=in0, in1=in1, op=op), reads=reads, writes=writes)

        def pe_mm(out, lhsT, rhs, start, stop, reads, writes):
            P.op("pe", lambda e: e.matmul(out, lhsT, rhs, start=start, stop=stop), reads=reads, writes=writes)

        def pe_tr(out, in_, ident, reads, writes):
            P.op("pe", lambda e: e.transpose(out, in_, ident), reads=reads, writes=writes)

        def pe_multi(specs, reads, writes):
            def fn(e):
                ins = None
                for (o_, l_, r_, st_, sp_) in specs:
                    ins = e.matmul(o_, l_, r_, start=st_, stop=sp_, skip_group_check=True)
                return ins
            P.op("pe", fn, reads=reads, writes=writes)

        def copy_any(i, out, in_, reads, writes):
            if i % 2 == 0:
                act(out, in_, AF.Copy, reads, writes)
            else:
                dve_copy(out, in_, reads, writes)

        def Hpage(t, kc):
            return t * 16 + kc

        def Hap(t, kc):
            return bfp("R1", Hpage(t, kc))

        def cond_of(t):
            return 0 if t < 2 else 1

        dma("sp", id32[:], ident_d.ap(), [], ["id32"])
        dma("pool", rm16[:], rmT_d.ap(), [], ["rm16"])
        dma("pool", id16[:], ident_d.ap(), [], ["id16"])
        dma("sp", condS[:], condT_d.ap(), [], ["condS"])
        dma("sp", gfT[:], gfT_d.ap(), [], ["gfT"])
        dma("sp", qkg[:], qkg_d.ap(), [], ["qkg"])
        P.op("dve", lambda e: e.memset(ones16[:], 1.0), reads=[], writes=["ones16"])
        act(scT[:], condS[:], AF.Silu, ["condS"], ["scT"])

        def adaln_gen(l):
            par = l % 2
            dma("sp", badaT[:], badaT_d[l], [], ["badaT"])
            dma("sp", g1T[:], g1T_d[l], [], ["g1T"])
            dma("sp", g2T[:], g2T_d[l], [], ["g2T"])
            dma("sp", convT[:], convT_d[l], [], ["convT"])
            b = psa.alloc()
            for j in range(96):
                w3, wk = wload(w_ada_d[l][:, j * 128:(j + 1) * 128], 16, 128, page=(None if l == 0 else 36 + 4 * (j % 3)))
                pairs = [(w3[:, kc, :], scT[:, 2 * kc:2 * kc + 2]) for kc in range(16)]
                mm_group(banks[b][:, 2 * j:2 * j + 2], pairs, wk + ["scT"], PSK(b))
                yield
            m3 = modT[:].rearrange("p (j c) -> p j c", c=2)
            b3 = banks[b][:, 0:192].rearrange("p (j c) -> p j c", c=2)
            for c in range(2):
                dve_tt(m3[:, :, c], b3[:, :, c], badaT[:], ALU.add, PSK(b) + ["badaT"], ["modT"])
            psa.release(b)
            for c in range(2):
                def mcol(which, c=c):
                    return m3[:, which * 16:(which + 1) * 16, c]
                A1 = AB[:, (0 * 2 + c) * 16:(0 * 2 + c) * 16 + 16]
                B1 = AB[:, (1 * 2 + c) * 16:(1 * 2 + c) * 16 + 16]
                A2 = AB[:, (2 * 2 + c) * 16:(2 * 2 + c) * 16 + 16]
                B2 = AB[:, (3 * 2 + c) * 16:(3 * 2 + c) * 16 + 16]
                G1 = GT[:, par * 64 + (0 * 2 + c) * 16:par * 64 + (0 * 2 + c) * 16 + 16]
                G2 = GT[:, par * 64 + (1 * 2 + c) * 16:par * 64 + (1 * 2 + c) * 16 + 16]
                dve_stt(A1, mcol(1), 1.0, g1T[:], ALU.add, ALU.mult, ["modT", "g1T"], ["AB"])
                dve_stt(A2, mcol(4), 1.0, g2T[:], ALU.add, ALU.mult, ["modT", "g2T"], ["AB"])
                dve_copy(B1, mcol(0), ["modT"], ["AB"])
                dve_copy(B2, mcol(3), ["modT"], ["AB"])
                dve_copy(G1, mcol(2), ["modT"], [("GT", par)])
                dve_copy(G2, mcol(5), ["modT"], [("GT", par)])

        def ABcol(which, c, e):
            o = (which * 2 + c) * 16 + e
            return AB[:, o:o + 1]

        def GTcol(par, which, c, e):
            o = par * 64 + (which * 2 + c) * 16 + e
            return GT[:, o:o + 1]

        def drain(gen, n=None):
            if gen is None:
                return
            k = 0
            while n is None or k < n:
                try:
                    next(gen)
                except StopIteration:
                    return
                k += 1

        def xs_ap(slot, arena="WS"):
            return ARENA[arena][:, slot * 16 * 512:(slot + 1) * 16 * 512].bitcast(F32).rearrange("p (e c) -> p e c", e=16)

        def norm_phase(mode, which=0):
            drain(norm_gen(mode, which))

        def norm_gen(mode, which=0, xs_arena="WS", intile=False):
            SQk = [("WS", 48), ("WS", 49)]
            SQ = [bfp("WS", 48)[:, 0:256], bfp("WS", 49)[:, 0:256]]
            RKs = [[("WS", 50)], [("WS", 51)]]
            Rrs = [f32p("WS", 50, 1), f32p("WS", 51, 1)]
            TMPk = [("R2", 48), ("R2", 49), ("R2", 50), ("R2", 51)]
            TMP = [f32p("R2", 48, 1), f32p("R2", 49, 1), f32p("R2", 50, 1), f32p("R2", 51, 1)]
            NH2 = 2 * NT

            def stage1(ht):
                t, hh = divmod(ht, 2)
                slot = ht % 3
                XS = xs_ap(slot, xs_arena)
                xk = pg(xs_arena, slot * 16, 16)
                xTk = [("xT", t, e) for e in range(16)]
                RK = RKs[ht % 2]
                Rr = Rrs[ht % 2]
                if mode == "in":
                    for cc in range(2):
                        chunk = 2 * ht + cc
                        rs = (chunk % 6)
                        XT = f32p("R2", rs * 8, 8)
                        dma("sp", XT, xin[chunk * 128:(chunk + 1) * 128, :], [], pg("R2", rs * 8, 8))
                    for e in range(16):
                        b = psa.alloc()
                        for cc in range(2):
                            chunk = 2 * ht + cc
                            rs = chunk % 6
                            XT = f32p("R2", rs * 8, 8)
                            pe_tr(banks[b][:, cc * 128:(cc + 1) * 128], XT[:, e * 128:(e + 1) * 128], id32[:],
                                  pg("R2", rs * 8, 8) + ["id32"], PSK(b))
                        copy_any(e, XS[:, e, :], banks[b][:, 0:256], PSK(b), xk)
                        psa.release(b)
                    dma("sp", xT[t][:, :, hh * 256:(hh + 1) * 256], XS, xk, xTk)
                else:
                    dma("sp", XS, xT[t][:, :, hh * 256:(hh + 1) * 256], xTk, xk)
                sb_ = psa.alloc()
                for e in range(16):
                    act(SQ[e % 2], XS[:, e, :], AF.Square, xk, [SQk[e % 2]])
                    pe_mm(banks[sb_][:, 0:256], ones16[:], SQ[e % 2], (e == 0), (e == 15), [SQk[e % 2], "ones16"], PSK(sb_))
                act(Rr, banks[sb_][:, 0:256], AF.Sqrt, PSK(sb_), RK, bias=EPS, scale=1.0 / D)
                psa.release(sb_)
                dve_recip(Rr, Rr, RK, RK)

            def stage2(ht):
                t, hh = divmod(ht, 2)
                c = cond_of(t)
                slot = ht % 3
                XS = xs_ap(slot, xs_arena)
                xk = pg(xs_arena, slot * 16, 16)
                RK = RKs[ht % 2]
                Rr = Rrs[ht % 2]
                if mode != "out":
                    for e in range(16):
                        A = ABcol(2 * which, c, e)
                        B = ABcol(2 * which + 1, c, e)
                        tm = TMP[e % 4]
                        dve_stt(tm, XS[:, e, :], A, Rr, ALU.mult, ALU.mult, xk + ["AB"] + RK, [TMPk[e % 4]])
                        dst = Hap(t, e)[:, hh * 256:(hh + 1) * 256]
                        if e % 4 == 3:
                            dve_ts(dst, tm, B, None, ALU.add, ALU.bypass, [TMPk[e % 4], "AB"], pg("R1", Hpage(t, e)))
                        else:
                            act(dst, tm, AF.Identity, [TMPk[e % 4], "AB"], pg("R1", Hpage(t, e)), bias=B, scale=1.0)
                else:
                    for e in range(16):
                        dve_stt(XS[:, e, :], XS[:, e, :], gfT[:, e:e + 1], Rr, ALU.mult, ALU.mult, xk + ["gfT"] + RK, xk)
                    for cc in range(2):
                        chunk = 2 * ht + cc
                        rs = chunk % 6
                        OT = f32p("R2", rs * 8, 8)
                        ok = pg("R2", rs * 8, 8)
                        for e4 in range(4):
                            b = psa.alloc()
                            for q in range(4):
                                e = e4 * 4 + q
                                pe_tr(banks[b][:, q * 128:(q + 1) * 128], XS[:, e, cc * 128:(cc + 1) * 128], id32[:], xk + ["id32"], PSK(b))
                            copy_any(e4, OT[:, e4 * 512:(e4 + 1) * 512], banks[b][:], PSK(b), ok)
                            psa.release(b)
                        dma("sp", yout[chunk * 128:(chunk + 1) * 128, :], OT, ok, [("yout", chunk)])

            if intile:
                for t_ in range(NT):
                    stage1(2 * t_)
                    stage1(2 * t_ + 1)
                    stage2(2 * t_)
                    stage2(2 * t_ + 1)
                    yield
                return
            stage1(0)
            for ht in range(NH2):
                if ht + 1 < NH2:
                    stage1(ht + 1)
                stage2(ht)
                if ht % 2 == 1:
                    yield

        def qk_norm(pb, t, gcol, st, out_bf, out_keys, i, out32=None, out32_keys=None):
            rope = t >= 2
            sqk, sq = st["sq"][i % 2]
            rk, rr = st["r"][i % 2]
            act(sq, banks[pb][:], AF.Square, PSK(pb), sqk)
            sb_ = psa.alloc()
            pe_mm(banks[sb_][:], ones16[:], sq, True, True, sqk + ["ones16"], PSK(sb_))
            act(rr, banks[sb_][:], AF.Sqrt, PSK(sb_), rk, bias=EPS, scale=1.0 / HD)
            psa.release(sb_)
            dve_recip(rr, rr, rk, rk)
            if not rope:
                if out32 is None:
                    dve_stt(out_bf, banks[pb][:], gcol, rr, ALU.mult, ALU.mult, PSK(pb) + ["qkg"] + rk, out_keys)
                else:
                    dve_stt(out32, banks[pb][:], gcol, rr, ALU.mult, ALU.mult, PSK(pb) + ["qkg"] + rk, out32_keys)
                    act(out_bf, out32, AF.Copy, out32_keys, out_keys)
                psa.release(pb)
                return
            q32k, q32 = st["qn"][i % 2]
            qnk, qn = st["qn16"][i % 2]
            t1k, t1 = st["t1"][0]
            ck_, cs_ = get_rope(t)
            dve_stt(qn, banks[pb][:], gcol, rr, ALU.mult, ALU.mult, PSK(pb) + ["qkg"] + rk, qnk)
            psa.release(pb)
            rb = psa.alloc()
            pe_mm(banks[rb][:], rm16[:], qn, True, True, qnk + ["rm16"], PSK(rb))
            dve_tt(t1, banks[rb][:], cs_[1], ALU.mult, PSK(rb) + ck_, t1k)
            psa.release(rb)
            pool_tt(q32, qn, cs_[0], ALU.mult, qnk + ck_, q32k)
            pool_tt(out_bf, q32, t1, ALU.add, q32k + t1k, out_keys)

        def qk_pipeline(units, st):
            n = len(units)
            S = [None] * n

            def stageA(u):
                un = units[u]
                pb = psa.alloc()
                S[u] = pb
                proj_group(un["w3"], un["wk"], un["t"], pb)
                sqk, sq = st["sq"][u % 3]
                act(sq, banks[pb][:], AF.Square, PSK(pb), sqk)

            def stageB(u):
                un = units[u]
                pb = S[u]
                t = un["t"]
                sqk, sq = st["sq"][u % 3]
                rk, rr = st["r"][u % 3]
                sb_ = psa.alloc()
                pe_mm(banks[sb_][:], ones16[:], sq, True, True, sqk + ["ones16"], PSK(sb_))
                act(rr, banks[sb_][:], AF.Sqrt, PSK(sb_), rk, bias=EPS, scale=1.0 / HD)
                psa.release(sb_)
                dve_recip(rr, rr, rk, rk)
                if t >= 2:
                    qnk, qn = st["qn16"][u % 3]
                    dve_stt(qn, banks[pb][:], un["gcol"], rr, ALU.mult, ALU.mult, PSK(pb) + ["qkg"] + rk, qnk)
                elif un["nk"] is None:
                    dve_stt(un["out_bf"], banks[pb][:], un["gcol"], rr, ALU.mult, ALU.mult, PSK(pb) + ["qkg"] + rk, un["out_keys"])
                else:
                    o32k, o32 = st["q32"][u % 2]
                    dve_stt(o32, banks[pb][:], un["gcol"], rr, ALU.mult, ALU.mult, PSK(pb) + ["qkg"] + rk, o32k)
                    act(un["out_bf"], o32, AF.Copy, o32k, un["out_keys"])
                psa.release(pb)

            def stageC(u):
                un = units[u]
                t = un["t"]
                if t >= 2:
                    qnk, qn = st["qn16"][u % 3]
                    t1k, t1 = st["t1"][u % 2]
                    q32k, q32 = st["q32"][u % 2]
                    ck_, cs_ = get_rope(t)
                    rb = psa.alloc()
                    pe_mm(banks[rb][:], rm16[:], qn, True, True, qnk + ["rm16"], PSK(rb))
                    act(t1, banks[rb][:], AF.Copy, PSK(rb), t1k)
                    psa.release(rb)
                    pool_tt(t1, t1, cs_[1], ALU.mult, t1k + ck_, t1k)
                    pool_tt(q32, qn, cs_[0], ALU.mult, qnk + ck_, q32k)
                    pool_tt(un["out_bf"], q32, t1, ALU.add, q32k + t1k, un["out_keys"])
                elif un["nk"] is not None:
                    l_, kvh = un["nk"]
                    o32k, o32 = st["q32"][u % 2]
                    tb = psa.alloc()
                    for c4 in range(4):
                        pe_tr(banks[tb][:, c4 * 128:(c4 + 1) * 128], o32[:, c4 * 128:(c4 + 1) * 128], id32[:], o32k + ["id32"], PSK(tb))
                    nkk, nks = st["t1"][1]
                    copy_any(1, nks, banks[tb][:], PSK(tb), nkk)
                    psa.release(tb)
                    for s2 in range(2):
                        dst = nk_d[2 * t + s2, l_, :, kvh * 128:(kvh + 1) * 128].rearrange("(h p) d -> p h d", p=128)
                        dma("sp", dst, nks[:, s2 * 256:(s2 + 1) * 256].rearrange("p (h d) -> p h d", h=2), nkk, [("nk", l_, t, kvh, s2)])

            for i in range(n + 2):
                if i < n:
                    stageA(i)
                if 0 <= i - 1 < n:
                    stageB(i - 1)
                if 0 <= i - 2 < n:
                    stageC(i - 2)

        rope_u = {"n": 0}

        def get_rope(t):
            sl = rope_u["n"] % 2
            rope_u["n"] += 1
            p = 44 + 4 * sl
            Cc = f32p("WS", p, 2)
            Ss = f32p("WS", p + 2, 2)
            dma("sp", Cc, ropeC_d[:, (t - 2) * 512:(t - 1) * 512], [], pg("WS", p, 2))
            dma("sp", Ss, ropeS_d[:, (t - 2) * 512:(t - 1) * 512], [], pg("WS", p + 2, 2))
            return pg("WS", p, 4), (Cc, Ss)

        def KT(kvh, tok0, n):
            o = kvh * KTOK + tok0
            return R2[:, o:o + n]

        def KTkeys(kvh, tok0, n):
            o = kvh * KTOK + tok0
            return pg("R2", o // 512, (o + n - 1) // 512 - o // 512 + 1)

        def Vap(c, kvh):
            return R2[:, (26 + c) * 512 + kvh * 128:(26 + c) * 512 + (kvh + 1) * 128]

        def tile_tok0(t):
            return t * 512

        def proj_group(w3, wk, t, pb):
            pairs = [(w3[:, kc, :], Hap(t, kc)) for kc in range(16)]
            mm_group(banks[pb][:], pairs, wk + pg("R1", t * 16, 16), PSK(pb))

        def phase2(l):
            wl = w_in_d[l]
            wstate["limit"] = 18
            wstate["pos"] = 0
            st = {
                "sq": [(pg("WS", 24 + i), bfp("WS", 24 + i)) for i in range(3)],
                "qn16": [(pg("WS", 27 + i), bfp("WS", 27 + i)) for i in range(3)],
                "r": [(pg("WS", 30 + 2 * i, 2), f32p("WS", 30 + 2 * i)) for i in range(3)],
                "q32": [(pg("WS", 36 + 2 * i, 2), f32p("WS", 36 + 2 * i)) for i in range(2)],
                "qn": [(pg("WS", 36 + 2 * i, 2), f32p("WS", 36 + 2 * i)) for i in range(2)],
                "t1": [(pg("WS", 40 + 2 * i, 2), f32p("WS", 40 + 2 * i)) for i in range(2)],
                "o": [(pg("WS", 42), bfp("WS", 42)), (pg("WS", 43), bfp("WS", 43))],
            }
            ckp = walloc(2)
            CK16 = WS[:, ckp * 512:(ckp + 2) * 512].rearrange("p (h c) -> p h c", h=2)
            dma("pool", CK16, ck_d[l].rearrange("(h p) c -> p h c", p=128), [], pg("WS", ckp, 2))
            Vc = R2[:, (26 + 24) * 512:(26 + 26) * 512].rearrange("p (h c) -> p h c", h=2)
            dma("pool", Vc, cv_d[l].rearrange("(h p) c -> p h c", p=128), [], pg("R2", 50, 2))
            b = psa.alloc()
            b16 = banks[b][:].bitcast(BF16)
            for kvh in range(KVH):
                for h2 in range(2):
                    o = (kvh * 2 + h2) * 128
                    pe_tr(b16[:, o:o + 128], CK16[:, h2, kvh * 128:(kvh + 1) * 128], id16[:], pg("WS", ckp, 2) + ["id16"], PSK(b))
            for kvh in range(KVH):
                dve_copy(KT(kvh, NTOK, PAST), b16[:, kvh * 256:(kvh + 1) * 256], PSK(b), KTkeys(kvh, NTOK, PAST))
            psa.release(b)
            if DBG.get("ph2", 99) < 1:
                return
            kg = qkg[:, 2 * l + 1:2 * l + 2]
            qg = qkg[:, 2 * l:2 * l + 1]
            units = []
            for kvh in range(KVH):
                w3, wk = wload(wl[:, C_K + kvh * 128:C_K + (kvh + 1) * 128], 16, 128)
                for t in range(NT):
                    units.append(dict(w3=w3, wk=wk, t=t, gcol=kg, out_bf=KT(kvh, t * 512, 512), out_keys=KTkeys(kvh, t * 512, 512),
                                      nk=(l, kvh) if t < 2 else None))
            qk_pipeline(units, st)
            if DBG.get("ph2", 99) < 2:
                return
            w3, wk = wload(wl[:, C_V:C_V + 512], 16, 512)
            vdbg = DBG.get("v", 99)
            for c in range(24):
                if vdbg < 1:
                    break
                t, c4 = divmod(c, 4)
                pb = psa.alloc()
                pairs = [(Hap(t, kc)[:, c4 * 128:(c4 + 1) * 128], w3[:, kc, :]) for kc in range(16)]
                mm_group(banks[pb][:], pairs, wk + pg("R1", t * 16, 16), PSK(pb))
                copy_any(c, bfp("R2", 26 + c), banks[pb][:], PSK(pb), pg("R2", 26 + c))
                if t < 2 and vdbg >= 2:
                    vk, v32 = st["qn"][c % 2]
                    copy_any(c + 1, v32, banks[pb][:], PSK(pb), vk)
                    seq, pos0 = divmod(c * 128, PSEQ)
                    if vdbg >= 3:
                        dma("sp", nv_d[seq, l, pos0:pos0 + 128, :], v32, vk, [("nv", l, c)])
                psa.release(pb)
            if DBG.get("ph2", 99) < 3:
                return
            scale = 1.0 / math.sqrt(HD)
            for h in range(NH):
                g = h // 4
                w3, wk = wload(wl[:, C_Q + h * 128:C_Q + (h + 1) * 128], 16, 128)
                units = [dict(w3=w3, wk=wk, t=t, gcol=qg, out_bf=bfp("WS", 18 + t), out_keys=pg("WS", 18 + t), nk=None) for t in range(NT)]
                qk_pipeline(units, st)
                attention_head(l, h, g, st, scale)

        def attn_evacuate(h, t, ob, db, st):
            o32k, o32 = st["qn"][t % 2]
            dk, d32 = st["r"][t % 2]
            dve_copy(o32, banks[ob][:], PSK(ob), o32k)
            psa.release(ob)
            dve_copy(d32, banks[db][:], PSK(db), dk)
            psa.release(db)
            dve_recip(d32, d32, dk, dk)
            osk, osb = st["o"][t % 2]
            dve_tt(osb, o32, d32, ALU.mult, o32k + dk, osk)
            dma("sp", Osp[t][:, h, :], osb, osk, [("O", t, h)])

        def pe_batch(items):
            specs = [it[0] for it in items]
            rd, wr = [], []
            for it in items:
                rd += it[1]
                wr += it[2]
            pe_multi(specs, rd, wr)

        def attention_head(l, h, g, st, scale):
            PTk = [pg("WS", 24 + i) for i in range(6)]
            PT = [bfp("WS", 24 + i) for i in range(6)]
            for t in range(2):
                Q = bfp("WS", 18 + t)
                Qkeys = pg("WS", 18 + t)
                ob = psa.alloc()
                db = psa.alloc()
                sbs = {}
                for j in range(2):
                    s_ = psa.alloc()
                    sbs[j] = s_
                    specs, rk = [], []
                    for a_ in range(2):
                        tok0 = (2 * t + a_) * PSEQ + j * 128
                        specs.append((banks[s_][:, a_ * 256:(a_ + 1) * 256], KT(g, tok0, 128), Q[:, a_ * 256:(a_ + 1) * 256], (a_ == 0), True))
                        rk += KTkeys(g, tok0, 128)
                    pe_multi(specs, rk + Qkeys, PSK(s_))
                for j in range(2):
                    s_ = sbs.pop(j)
                    pt, ptk = PT[j], PTk[j]
                    act(pt, banks[s_][:], AF.Exp, PSK(s_), ptk, scale=scale)
                    psa.release(s_)
                    specs, rk = [], []
                    for a_ in range(2):
                        vc = (2 * t + a_) * 2 + j
                        specs.append((banks[ob][:, a_ * 256:(a_ + 1) * 256], Vap(vc, g), pt[:, a_ * 256:(a_ + 1) * 256], (j == 0 and a_ == 0), (j == 1)))
                        rk += pg("R2", 26 + vc)
                    specs.append((banks[db][:], ones16[:], pt, (j == 0), (j == 1)))
                    pe_multi(specs, rk + ptk + ["ones16"], PSK(ob) + PSK(db))
                attn_evacuate(h, t, ob, db, st)
            NP = 9
            stream = [(t, p) for t in range(2, NT) for p in range(NP)]
            sbs = {}

            def S_item(t, j):
                s_ = psa.alloc()
                sbs[(t, j)] = s_
                tok0 = 1024 + j * 128
                return ((banks[s_][:], KT(g, tok0, 128), bfp("WS", 18 + t), True, True), KTkeys(g, tok0, 128) + pg("WS", 18 + t), PSK(s_))
            pe_batch([S_item(t, j) for (t, p) in stream[:2] for j in (2 * p, 2 * p + 1)])
            obdb = {}
            for idx, (t, p) in enumerate(stream):
                if t not in obdb:
                    obdb[t] = (psa.alloc(), psa.alloc())
                if p == NP - 3 and t + 1 < NT:
                    obdb[t + 1] = (psa.alloc(), psa.alloc())
                ob, db = obdb[t]
                items = []
                for jj, j in enumerate((2 * p, 2 * p + 1)):
                    s_ = sbs.pop((t, j))
                    k = (2 * idx + jj) % 6
                    pt, ptk = PT[k], PTk[k]
                    act(pt, banks[s_][:], AF.Exp, PSK(s_), ptk, scale=scale)
                    psa.release(s_)
                    vc = 8 + j
                    items.append(((banks[ob][:], Vap(vc, g), pt, (j == 0), (j == 17)), pg("R2", 26 + vc) + ptk, PSK(ob)))
                    items.append(((banks[db][:], ones16[:], pt, (j == 0), (j == 17)), ptk + ["ones16"], PSK(db)))
                if idx + 2 < len(stream):
                    t2, p2 = stream[idx + 2]
                    pe_batch([S_item(t2, 2 * p2), S_item(t2, 2 * p2 + 1)])
                pe_batch(items)
                if p == NP - 1:
                    attn_evacuate(h, t, ob, db, st)

        def phase2_conv_gates(l):
            wl = w_in_d[l]
            wstate["limit"] = WRING
            wstate["pos"] = 0
            CB = [(pg("WS", 24 + 2 * i, 2), f32p("WS", 24 + 2 * i)) for i in range(3)]
            PR = [(pg("WS", 30 + 2 * i, 2), f32p("WS", 30 + 2 * i)) for i in range(3)]
            CS = [(pg("WS", 36 + 2 * i, 2), f32p("WS", 36 + 2 * i)) for i in range(2)]
            ACk, AC = pg("WS", 40, 2), f32p("WS", 40)
            u = 0
            for j in range(8):
                def cw(k, j=j):
                    return convT[:, j * 4 + k:j * 4 + k + 1]
                wpage = walloc(12)
                wb3, wbk = wload(wl[:, C_CB + j * 128:C_CB + (j + 1) * 128], 16, 128, page=wpage)
                wc3, wck = wload(wl[:, C_CC + j * 128:C_CC + (j + 1) * 128], 16, 128, page=wpage + 4)
                wx3, wxk = wload(wl[:, C_CX + j * 128:C_CX + (j + 1) * 128], 16, 128, page=wpage + 8)

                def conv_tile(t, j=j, cw=cw):
                    cbk, cb = CB[t % 3]
                    prk, pr = PR[t % 3]
                    dve_ts(AC, pr, cw(1), cw(3), ALU.mult, ALU.add, prk + ["convT"], ACk)
                    segs = [(0, 256), (256, 512)] if t < 2 else [(0, 512)]
                    for (a, b_) in segs:
                        dve_stt(AC[:, a + 1:b_], pr[:, a:b_ - 1], cw(0), AC[:, a + 1:b_], ALU.mult, ALU.add, prk + ACk + ["convT"], ACk)
                        dve_stt(AC[:, a:b_ - 1], pr[:, a + 1:b_], cw(2), AC[:, a:b_ - 1], ALU.mult, ALU.add, prk + ACk + ["convT"], ACk)
                    if t > 2:
                        pk2, pr2 = PR[(t - 1) % 3]
                        dve_stt(AC[:, 0:1], pr2[:, 511:512], cw(0), AC[:, 0:1], ALU.mult, ALU.add, pk2 + ACk + ["convT"], ACk)
                    if 2 <= t < NT - 1:
                        pk2, pr2 = PR[(t + 1) % 3]
                        dve_stt(AC[:, 511:512], pr2[:, 0:1], cw(2), AC[:, 511:512], ALU.mult, ALU.add, pk2 + ACk + ["convT"], ACk)
                    dve_tt(bfp("R2", j * 6 + t), cb, AC, ALU.mult, cbk + ACk, pg("R2", j * 6 + t))

                for t in range(NT):
                    cbk, cb = CB[t % 3]
                    prk, pr = PR[t % 3]
                    csk, cs = CS[u % 2]
                    pb = psa.alloc()
                    proj_group(wb3, wbk, t, pb)
                    act(cb, banks[pb][:], AF.Copy, PSK(pb), cbk)
                    psa.release(pb)
                    pb = psa.alloc()
                    proj_group(wc3, wck, t, pb)
                    act(cs, banks[pb][:], AF.Copy, PSK(pb), csk)
                    psa.release(pb)
                    pb = psa.alloc()
                    proj_group(wx3, wxk, t, pb)
                    dve_tt(pr, banks[pb][:], cs, ALU.mult, PSK(pb) + csk, prk)
                    psa.release(pb)
                    u += 1
                    if t >= 1:
                        conv_tile(t - 1)
                conv_tile(NT - 1)
            SGk = [pg("WS", 24 + i) for i in range(4)]
            SG = [bfp("WS", 24 + i) for i in range(4)]
            u = 0
            for e2 in range(32):
                which, e = divmod(e2, 16)
                w3, wk = wload(wl[:, C_GA + e2 * 128:C_GA + (e2 + 1) * 128], 16, 128)
                for t in range(NT):
                    pb = psa.alloc()
                    proj_group(w3, wk, t, pb)
                    act(SG[u % 4], banks[pb][:], AF.Sigmoid, PSK(pb), SGk[u % 4])
                    psa.release(pb)
                    dma("sp", Gsp[t][:, 2 * e + which, :], SG[u % 4], SGk[u % 4], [("G", t, 2 * e + which)])
                    u += 1

        def load_R1(src, nchunk, t_list, keyname):
            for t in t_list:
                tt = t_list.index(t)
                nparts = 4 if nchunk > 16 else 2
                for half in range(nparts):
                    n2 = nchunk // nparts
                    c0 = half * n2
                    dst = R1[:, (tt * nchunk + c0) * 512:(tt * nchunk + c0 + n2) * 512].rearrange("p (c k) -> p c k", c=n2)
                    dma("sp", dst, src[t][:, c0:c0 + n2, :], [(keyname, t, c) for c in range(c0, c0 + n2)], pg("R1", tt * nchunk + c0, n2))

        def phase3(l):
            load_R1(Osp, 16, list(range(NT)), "O")
            GLk = [pg("WS", 24 + 2 * i, 2) for i in range(3)]
            GL = [bfp("WS", 24 + 2 * i, 2).rearrange("p (w c) -> p w c", w=2) for i in range(3)]
            T1 = [(pg("WS", 30 + 2 * i, 2), f32p("WS", 30 + 2 * i)) for i in range(2)]
            T2 = [(pg("WS", 34 + 2 * i, 2), f32p("WS", 34 + 2 * i)) for i in range(2)]
            MOk = [pg("WS", 38 + i) for i in range(3)]
            MO = [bfp("WS", 38 + i) for i in range(3)]
            units = [(e, t) for e in range(16) for t in range(NT)]
            PF = 2

            def gload(i):
                e, t = units[i]
                dma("sp", GL[i % 3], Gsp[t][:, 2 * e:2 * e + 2, :], [("G", t, 2 * e), ("G", t, 2 * e + 1)], GLk[i % 3])
            for i in range(PF):
                gload(i)
            wa3 = wb3 = None
            for i, (e, t) in enumerate(units):
                if t == 0:
                    wp = walloc(6)
                    wa3, wak = wload(w_a_d[l][:, e * 128:(e + 1) * 128], 8, 128, page=wp)
                    wb3, wbk = wload(w_b_d[l][:, e * 128:(e + 1) * 128], 16, 128, page=wp + 2)
                if i + PF < len(units):
                    gload(i + PF)
                ua = psa.alloc()
                pairs = [(wa3[:, kc, :], bfp("R2", kc * 6 + t)) for kc in range(8)]
                mm_group(banks[ua][:], pairs, wak + [("R2", kc * 6 + t) for kc in range(8)], PSK(ua))
                ub = psa.alloc()
                pairs = [(wb3[:, kc, :], Hap(t, kc)) for kc in range(16)]
                mm_group(banks[ub][:], pairs, wbk + pg("R1", t * 16, 16), PSK(ub))
                t1k, t1 = T1[i % 2]
                t2k, t2 = T2[i % 2]
                gl = GL[i % 3]
                dve_tt(t1, banks[ua][:], gl[:, 0, :], ALU.mult, PSK(ua) + GLk[i % 3], t1k)
                psa.release(ua)
                dve_tt(t2, banks[ub][:], gl[:, 1, :], ALU.mult, PSK(ub) + GLk[i % 3], t2k)
                psa.release(ub)
                mo = MO[i % 3]
                dve_tt(mo, t1, t2, ALU.add, t1k + t2k, MOk[i % 3])
                dma("sp", Msp[t][:, e, :], mo, MOk[i % 3], [("M", t, e)])

        def resid_phase(l, wsrc, kc_n, which_gt, t_list, rslot, e_list=None, tile_major=False, after_tile=None):
            XL = [(pg("WS", 24 + 2 * i, 2), f32p("WS", 24 + 2 * i)) for i in range(4)]
            if e_list is None:
                e_list = list(range(16))
            if tile_major:
                units = [(e, t) for t in t_list for e in e_list]
            else:
                units = [(e, t) for e in e_list for t in t_list]
            PF = 2

            def xload(i):
                e, t = units[i]
                dma("sp", XL[i % 4][1], xT[t][:, e, :], [("xT", t, e)], XL[i % 4][0])
            for i in range(min(PF, len(units))):
                xload(i)
            wcache = {}
            for i, (e, t) in enumerate(units):
                if e not in wcache:
                    wcache[e] = wload(wsrc[:, e * 128:(e + 1) * 128], kc_n, 128)
                w3, wk = wcache[e]
                if i + PF < len(units):
                    xload(i + PF)
                tt = t_list.index(t)
                pb = psa.alloc()
                nsub = 4 if kc_n > 16 else 1
                step = kc_n // nsub
                for sg in range(nsub):
                    k0, k1 = sg * step, (sg + 1) * step
                    pairs = [(w3[:, kc, :], bfp("R1", tt * kc_n + kc)) for kc in range(k0, k1)]
                    mm_group(banks[pb][:], pairs, wk + pg("R1", tt * kc_n + k0, k1 - k0), PSK(pb), first=(sg == 0), last=(sg == nsub - 1))
                xk, xs = XL[i % 4]
                gcol = GTcol(l % 2, which_gt, cond_of(t), e)
                dve_stt(xs, banks[pb][:], gcol, xs, ALU.mult, ALU.add, PSK(pb) + xk + [("GT", l % 2)], xk)
                psa.release(pb)
                dma("sp", xT[t][:, e, :], xs, xk, [("xT", t, e)])
                if tile_major and after_tile is not None and e == e_list[-1]:
                    after_tile(t)

        def phase4(l):
            load_R1(Msp, 16, list(range(NT)), "M")
            tl = list(range(NT))
            if not DBG.get("ovl4", 1):
                resid_phase(l, w_o_d[l], 16, 0, tl, 0)
                norm_phase("mid", which=1)
                return
            NTAIL = 4
            resid_phase(l, w_o_d[l], 16, 0, tl, 0, e_list=list(range(16 - NTAIL)))
            wstate["pos"] = 0
            ng = norm_gen("mid", 1, "R2", intile=True)
            resid_phase(l, w_o_d[l], 16, 0, tl, 0, e_list=list(range(16 - NTAIL, 16)), tile_major=True, after_tile=lambda t: next(ng))
            drain(ng)

        def phase6(l, bg=None, normgen=None):
            SIk = [pg("WS", 24 + 2 * i, 2) for i in range(3)]
            SI = [f32p("WS", 24 + 2 * i) for i in range(3)]
            AOk = [pg("WS", 30 + i) for i in range(4)]
            AO = [bfp("WS", 30 + i) for i in range(4)]
            ust = {"u": 0}
            blocks = {}

            def load(f):
                wp = walloc(8)
                wg3, wgk = wload(w_gate_d[l][:, f * 128:(f + 1) * 128], 16, 128, page=wp)
                wu3, wuk = wload(w_up_d[l][:, f * 128:(f + 1) * 128], 16, 128, page=wp + 4)
                blocks[f] = (wg3, wgk, wu3, wuk)

            def unit(f, t):
                wg3, wgk, wu3, wuk = blocks[f]
                u = ust["u"]
                gb = psa.alloc()
                proj_group(wg3, wgk, t, gb)
                ub = psa.alloc()
                proj_group(wu3, wuk, t, ub)
                si = SI[u % 3]
                act(si, banks[gb][:], AF.Silu, PSK(gb), SIk[u % 3])
                psa.release(gb)
                ao = AO[u % 4]
                dve_tt(ao, banks[ub][:], si, ALU.mult, PSK(ub) + SIk[u % 3], AOk[u % 4])
                psa.release(ub)
                dma("sp", Asp[t][:, f, :], ao, AOk[u % 4], [("A", t, f)])
                ust["u"] = u + 1

            f0 = 0
            if normgen is not None:
                wstate["pos"] = 0
                f0 = 3
                for f in range(f0):
                    load(f)
                for t in range(NT):
                    next(normgen)
                    for f in range(f0):
                        unit(f, t)
                drain(normgen)
            for f in range(f0, NF):
                load(f)
                for t in range(NT):
                    unit(f, t)
                drain(bg, 3)
            drain(bg)

        phases = []
        for l in range(depth):
            if l == 0:
                phases.append(lambda: drain(adaln_gen(0)))
            phases.append(lambda l=l: norm_phase("in" if l == 0 else "mid", which=0))
            phases.append(lambda l=l: phase2(l))
            phases.append(lambda l=l: phase2_conv_gates(l))
            phases.append(lambda l=l: phase3(l))
            phases.append(lambda l=l: phase4(l))
            phases.append(lambda l=l: phase6(l, adaln_gen(l + 1) if l + 1 < depth else None))

            def ph7(l=l):
                for third in range(3):
                    tl = [2 * third, 2 * third + 1]
                    load_R1(Asp, NF, tl, "A")
                    resid_phase(l, w_down_d[l], NF, 1, tl, 0)
            phases.append(ph7)
        phases.append(lambda: norm_phase("out"))
        for i, ph in enumerate(phases):
            if maxph is not None and i >= maxph and i != len(phases) - 1:
                continue
            ph()

        emit(nc, P, sems)
    return nc


def _rope_tables():
    rows = STOK // 64
    row = np.repeat(np.arange(rows, dtype=np.float32), 64)
    col = np.tile(np.arange(64, dtype=np.float32), rows)
    n_freq = HD // 4
    inv = (np.float32(10000.0) ** (-np.arange(n_freq, dtype=np.float32) / np.float32(n_freq))).astype(np.float32)
    ang = np.concatenate([row[:, None] * inv, col[:, None] * inv], axis=-1).astype(np.float32)
    cos = np.cos(ang).astype(np.float32)
    sin = np.sin(ang).astype(np.float32)
    C = np.ascontiguousarray(np.concatenate([cos, cos], axis=1).T)
    S = np.ascontiguousarray(np.concatenate([sin, sin], axis=1).T)
    return C, S


def _rot_matrix_T():
    Rm = np.zeros((128, 128), np.float32)
    for d in range(64):
        Rm[d, d + 64] = -1.0
        Rm[d + 64, d] = 1.0
    return np.ascontiguousarray(Rm.T)


def make_in_maps(inp, depth=DEPTH):
    f = lambda a: np.ascontiguousarray(np.asarray(a, dtype=np.float32))
    x_prompt, x_sample = f(inp["x_prompt"]), f(inp["x_sample"])
    cache_k, cache_v = f(inp["cache_k"]), f(inp["cache_v"])
    c, c_ctx = f(inp["c"]), f(inp["c_ctx"])
    C, S = _rope_tables()
    shared = {
        "w_ada": f(inp["w_ada"][:depth]), "w_in": f(inp["w_in"][:depth]), "w_a": f(inp["w_a"][:depth]), "w_b": f(inp["w_b"][:depth]),
        "w_o": f(inp["w_o"][:depth]), "w_gate": f(inp["w_gate"][:depth]), "w_up": f(inp["w_up"][:depth]), "w_down": f(inp["w_down"][:depth]),
        "badaT": np.ascontiguousarray(f(inp["b_ada"]).reshape(DEPTH, 96, 128).transpose(0, 2, 1)),
        "g1T": np.ascontiguousarray(f(inp["norm1"]).reshape(DEPTH, 16, 128).transpose(0, 2, 1)),
        "g2T": np.ascontiguousarray(f(inp["norm2"]).reshape(DEPTH, 16, 128).transpose(0, 2, 1)),
        "gfT": np.ascontiguousarray(f(inp["norm_f"]).reshape(16, 128).T),
        "ropeC": C, "ropeS": S,
        "ident": np.eye(128, dtype=np.float32),
        "rmT": _rot_matrix_T(),
    }
    qkg = np.zeros((128, 2 * DEPTH), np.float32)
    for l in range(DEPTH):
        qkg[:, 2 * l] = inp["q_gain"][l]
        qkg[:, 2 * l + 1] = inp["k_gain"][l]
    shared["qkg"] = qkg
    cw = f(inp["conv_w"]).reshape(DEPTH, 3, 8, 128)
    cb = f(inp["conv_b"]).reshape(DEPTH, 8, 128)
    convT = np.zeros((DEPTH, 128, 8, 4), np.float32)
    convT[:, :, :, 0:3] = cw.transpose(0, 3, 2, 1)
    convT[:, :, :, 3] = cb.transpose(0, 2, 1)
    shared["convT"] = np.ascontiguousarray(convT.reshape(DEPTH, 128, 32))
    maps = []
    for i in range(NCORE):
        m = dict(shared)
        m["xin"] = np.ascontiguousarray(np.concatenate(
            [x_prompt[NPSEQ * i:NPSEQ * (i + 1)].reshape(NPSEQ * PSEQ, D), x_sample[i]], axis=0))
        m["ck"] = np.ascontiguousarray(cache_k[i].reshape(DEPTH, PAST, 512))
        m["cv"] = np.ascontiguousarray(cache_v[i].reshape(DEPTH, PAST, 512))
        cond = np.stack([c_ctx, c[i]], axis=0)
        m["condT"] = np.ascontiguousarray(cond.reshape(2, 16, 128).transpose(2, 1, 0).reshape(128, 32))
        maps.append(m)
    return maps


_NC_CACHE = {}


def kernel(**inputs):
    if "nc" not in _NC_CACHE:
        _NC_CACHE["nc"] = build(DEPTH)
    nc = _NC_CACHE["nc"]
    maps = make_in_maps(inputs)
    res = run_bass_kernel_spmd(nc, maps, core_ids=list(range(NCORE)))
    B = NCORE * NPSEQ
    y_prompt = np.empty((B, PSEQ, D), np.float32)
    y_sample = np.empty((NCORE, STOK, D), np.float32)
    nk = np.empty((B, DEPTH, PSEQ, KVH, HD), np.float32)
    nv = np.empty((B, DEPTH, PSEQ, KVH, HD), np.float32)
    for i in range(NCORE):
        r = res.results[i]
        y = r["yout"]
        y_prompt[NPSEQ * i:NPSEQ * (i + 1)] = y[:NPSEQ * PSEQ].reshape(NPSEQ, PSEQ, D)
        y_sample[i] = y[NPSEQ * PSEQ:]
        nk[NPSEQ * i:NPSEQ * (i + 1)] = r["nk"].reshape(NPSEQ, DEPTH, PSEQ, KVH, HD)
        nv[NPSEQ * i:NPSEQ * (i + 1)] = r["nv"].reshape(NPSEQ, DEPTH, PSEQ, KVH, HD)
    return (y_prompt, y_sample, nk, nv)
```

```python
import math
from contextlib import ExitStack

import numpy as np
import concourse.bass as bass
import concourse.mybir as mybir
from concourse.bass_utils import run_bass_kernel_spmd

F32 = mybir.dt.float32
BF16 = mybir.dt.bfloat16
AF = mybir.ActivationFunctionType
ALU = mybir.AluOpType

D = 2048
DEPTH = 4
NCORE = 8
PSEQ = 256
NPSEQ = 4
STOK = 2048
PAST = 256
NTOK = NPSEQ * PSEQ + STOK
NT = NTOK // 512
KVH = 4
NH = 16
HD = 128
D_CONV = 1024
D_FF = 5632
NF = D_FF // 128
IN_COLS = 10240
EPS = 1e-6
C_Q, C_K, C_V, C_CB, C_CC, C_CX, C_GA, C_GB = 0, 2048, 2560, 3072, 4096, 5120, 6144, 8192
KTOK = NTOK + PAST

DBG = {}
ENGS = ("pe", "act", "dve", "pool", "sp")
NDMA = 12


class Plan:
    def __init__(self):
        self.ops = {e: [] for e in ENGS}
        self.count = {e: 0 for e in ENGS}
        self.waited = {e: {} for e in ENGS}
        self.last_w = {}
        self.reads = {}
        self.dq = {q: {"n": 0, "vals": [0] * NDMA} for q in ("sp", "pool")}

    def op(self, eng, fn, reads=(), writes=(), dma=False):
        deps = {}

        def add(tok):
            if tok is None:
                return
            k, v, e = tok
            if e == "pe" and eng == "pe":
                return
            if deps.get(k, 0) < v:
                deps[k] = v

        lw = self.last_w
        xr = [p for p in reads if isinstance(p, tuple) and p[0] == "ps"]
        if xr:
            reads = [p for p in reads if not (isinstance(p, tuple) and p[0] == "ps")]
            writes = list(writes) + [p for p in xr if p not in writes]
        for p in reads:
            add(lw.get(p))
        for p in writes:
            add(lw.get(p))
            rd = self.reads.get(p)
            if rd:
                for k, (v, e) in rd.items():
                    add((k, v, e))
        if dma:
            pool = self.dq[eng]
            idx = pool["n"] % NDMA
            pool["n"] += 1
            prev = pool["vals"][idx]
            if prev > 0:
                add((("dma", eng, idx), prev, None))
            pool["vals"][idx] = prev + 16
            mykey = ("dma", eng, idx)
            myval = prev + 16
            tok_e = None
        else:
            self.count[eng] += 1
            mykey = ("eng", eng)
            myval = self.count[eng]
            tok_e = eng
        waits = []
        wd = self.waited[eng]
        for k, v in deps.items():
            if wd.get(k, 0) >= v:
                continue
            wd[k] = v
            waits.append((k, v))
        self.ops[eng].append((waits, fn, mykey, dma))
        tok = (mykey, myval, tok_e)
        for p in writes:
            lw[p] = tok
            self.reads[p] = {}
        for p in reads:
            d = self.reads.get(p)
            if d is None:
                d = self.reads[p] = {}
            d[mykey] = (myval, tok_e)
        return tok

    def final_waits(self, eng):
        waits = []
        for e in ENGS:
            if self.count[e] > 0 and e != eng:
                waits.append((("eng", e), self.count[e]))
        for q, pool in self.dq.items():
            for idx, v in enumerate(pool["vals"]):
                if v > 0:
                    waits.append((("dma", q, idx), v))
        return waits


def all_semkeys():
    ks = [("eng", e) for e in ENGS]
    for q in ("sp", "pool"):
        for i in range(NDMA):
            ks.append(("dma", q, i))
    return ks


def emit(nc, plan, sems):
    fin = plan.final_waits("sp")
    with nc.Block() as block:
        def run(engname, e):
            for waits, fn, mykey, dma in plan.ops[engname]:
                for k, v in waits:
                    e.wait_ge(sems[k], v)
                ins = fn(e)
                ins.then_inc(sems[mykey], 16 if dma else 1)

        @block.tensor
        def _(e):
            run("pe", e)

        @block.scalar
        def _(e):
            run("act", e)

        @block.vector
        def _(e):
            run("dve", e)

        @block.gpsimd
        def _(e):
            run("pool", e)

        @block.sync
        def _(e):
            run("sp", e)
            for k, v in fin:
                e.wait_ge(sems[k], v)


R1P, R2P, WSP = 96, 52, 52
WRING = 24
STB = 24


def build(depth=DEPTH, maxph=None):
    nc = bass.Bass("TRN2", target_bir_lowering=False)

    def din(name, shape):
        return nc.dram_tensor(name, list(shape), F32, kind="ExternalInput")

    xin = din("xin", [NTOK, D])
    ck_d = din("ck", [DEPTH, PAST, 512])
    cv_d = din("cv", [DEPTH, PAST, 512])
    condT_d = din("condT", [128, 32])
    w_ada_d = din("w_ada", [depth, D, 6 * D])
    badaT_d = din("badaT", [DEPTH, 128, 96])
    g1T_d = din("g1T", [DEPTH, 128, 16])
    g2T_d = din("g2T", [DEPTH, 128, 16])
    gfT_d = din("gfT", [128, 16])
    qkg_d = din("qkg", [128, 2 * DEPTH])
    convT_d = din("convT", [DEPTH, 128, 32])
    w_in_d = din("w_in", [depth, D, IN_COLS])
    w_a_d = din("w_a", [depth, D_CONV, D])
    w_b_d = din("w_b", [depth, D, D])
    w_o_d = din("w_o", [depth, D, D])
    w_gate_d = din("w_gate", [depth, D, D_FF])
    w_up_d = din("w_up", [depth, D, D_FF])
    w_down_d = din("w_down", [depth, D_FF, D])
    ropeC_d = din("ropeC", [128, STOK])
    ropeS_d = din("ropeS", [128, STOK])
    ident_d = din("ident", [128, 128])
    rmT_d = din("rmT", [128, 128])

    yout = nc.dram_tensor("yout", [NTOK, D], F32, kind="ExternalOutput")
    nk_d = nc.dram_tensor("nk", [NPSEQ, DEPTH, PSEQ, 512], F32, kind="ExternalOutput")
    nv_d = nc.dram_tensor("nv", [NPSEQ, DEPTH, PSEQ, 512], F32, kind="ExternalOutput")

    xT = nc.dram_tensor("xT_s", [NT, 128, 16, 512], F32)
    Osp = nc.dram_tensor("O_s", [NT, 128, 16, 512], BF16)
    Gsp = nc.dram_tensor("G_s", [NT, 128, 32, 512], BF16)
    Msp = nc.dram_tensor("M_s", [NT, 128, 16, 512], BF16)
    Asp = nc.dram_tensor("A_s", [NT, 128, NF, 512], BF16)

    es = ExitStack()
    with es:
        def sb(name, shape, dt):
            return es.enter_context(nc.sbuf_tensor("sb_" + name, list(shape), dt))

        R1 = sb("R1", [128, R1P * 512], BF16)
        R2 = sb("R2", [128, R2P * 512], BF16)
        WS = sb("WS", [128, WSP * 512], BF16)
        ARENA = {"R1": R1, "R2": R2, "WS": WS}

        ones16 = sb("ones16", [128, 128], BF16)
        id32 = sb("id32", [128, 128], F32)
        id16 = sb("id16", [128, 128], BF16)
        rm16 = sb("rm16", [128, 128], BF16)
        condS = sb("condS", [128, 32], F32)
        scT = sb("scT", [128, 32], BF16)
        badaT = sb("badaT", [128, 96], F32)
        modT = sb("modT", [128, 192], F32)
        g1T = sb("g1T", [128, 16], F32)
        g2T = sb("g2T", [128, 16], F32)
        gfT = sb("gfT", [128, 16], F32)
        AB = sb("AB", [128, 4 * 2 * 16], F32)
        GT = sb("GT", [128, 2 * 2 * 2 * 16], F32)
        qkg = sb("qkg", [128, 2 * DEPTH], F32)
        convT = sb("convT", [128, 32], F32)

        banks = [es.enter_context(nc.psum_tensor(f"ps{i}", [128, 512], F32)) for i in range(8)]
        sems = {k: es.enter_context(nc.semaphore("s_" + "_".join(str(x) for x in k))) for k in all_semkeys()}

        P = Plan()

        def pg(arena, first, n=1):
            return [(arena, p) for p in range(first, first + n)]

        def bfp(arena, page, n=1):
            return ARENA[arena][:, page * 512:(page + n) * 512]

        def f32p(arena, page, n=2):
            return ARENA[arena][:, page * 512:(page + n) * 512].bitcast(F32)

        class PsumAlloc:
            def __init__(self):
                self.free = list(range(8))

            def alloc(self):
                assert self.free, "psum exhausted"
                return self.free.pop(0)

            def release(self, b):
                self.free.append(b)

        psa = PsumAlloc()

        def PSK(b):
            return [("ps", b)]

        wstate = {"pos": 0, "limit": WRING}

        def walloc(n):
            if wstate["pos"] + n > wstate["limit"]:
                wstate["pos"] = 0
            p = wstate["pos"]
            wstate["pos"] += n
            return p

        def dma(eng, out, in_, reads, writes):
            P.op(eng, lambda e: e.dma_start(out=out, in_=in_), reads=reads, writes=writes, dma=True)

        def wload(src2d, kc, ncols, page=None):
            npages = (kc * ncols + 511) // 512
            if page is None:
                page = walloc(npages)
            dst = WS[:, page * 512: page * 512 + kc * ncols].rearrange("p (k c) -> p k c", k=kc)
            keys = pg("WS", page, npages)
            dma("pool", dst, src2d.rearrange("(k p) c -> p k c", p=128), [], keys)
            return dst, keys

        def mm_group(bank_ap, pairs, reads, bank_keys, skip=False, first=True, last=True):
            n = len(pairs)

            def fn(e):
                ins = None
                for i, (l, r) in enumerate(pairs):
                    ins = e.matmul(bank_ap, l, r, start=(first and i == 0), stop=(last and i == n - 1))
                return ins
            P.op("pe", fn, reads=reads, writes=bank_keys)

        def act(out, in_, func, reads, writes, bias=0.0, scale=1.0):
            P.op("act", lambda e: e.activation(out=out, in_=in_, func=func, bias=bias, scale=scale), reads=reads, writes=writes)

        def dve_tt(out, in0, in1, op, reads, writes):
            P.op("dve", lambda e: e.tensor_tensor(out=out, in0=in0, in1=in1, op=op), reads=reads, writes=writes)

        def dve_stt(out, in0, scalar, in1, op0, op1, reads, writes):
            P.op("dve", lambda e: e.scalar_tensor_tensor(out=out, in0=in0, scalar=scalar, in1=in1, op0=op0, op1=op1), reads=reads, writes=writes)

        def dve_ts(out, in0, s1, s2, op0, op1, reads, writes):
            if s2 is None:
                P.op("dve", lambda e: e.tensor_scalar(out=out, in0=in0, scalar1=s1, scalar2=None, op0=op0), reads=reads, writes=writes)
            else:
                P.op("dve", lambda e: e.tensor_scalar(out=out, in0=in0, scalar1=s1, scalar2=s2, op0=op0, op1=op1), reads=reads, writes=writes)

        def dve_copy(out, in_, reads, writes):
            P.op("dve", lambda e: e.tensor_copy(out=out, in_=in_), reads=reads, writes=writes)

        def dve_recip(out, in_, reads, writes):
            P.op("dve", lambda e: e.reciprocal(out=out, in_=in_), reads=reads, writes=writes)

        def dve_recip2(x, scr, reads, writes):
            P.op("dve", lambda e: e.reciprocal_approx_accurate(out=x, in_=x, scratch=scr), reads=reads, writes=writes)

        def pool_tt(out, in0, in1, op, reads, writes):
            P.op("pool", lambda e: e.tensor_tensor(out=out, in0=in0, in1=in1, op=op), reads=reads, writes=writes)

        def pe_mm(out, lhsT, rhs, start, stop, reads, writes):
            P.op("pe", lambda e: e.matmul(out, lhsT, rhs, start=start, stop=stop), reads=reads, writes=writes)

        def pe_tr(out, in_, ident, reads, writes):
            P.op("pe", lambda e: e.transpose(out, in_, ident), reads=reads, writes=writes)

        def pe_multi(specs, reads, writes):
            def fn(e):
                ins = None
                for (o_, l_, r_, st_, sp_) in specs:
                    ins = e.matmul(o_, l_, r_, start=st_, stop=sp_, skip_group_check=True)
                return ins
            P.op("pe", fn, reads=reads, writes=writes)

        def copy_any(i, out, in_, reads, writes):
            if i % 2 == 0:
                act(out, in_, AF.Copy, reads, writes)
            else:
                dve_copy(out, in_, reads, writes)

        def Hpage(t, kc):
            return t * 16 + kc

        def Hap(t, kc):
            return bfp("R1", Hpage(t, kc))

        def cond_of(t):
            return 0 if t < 2 else 1

        dma("sp", id32[:], ident_d.ap(), [], ["id32"])
        dma("pool", rm16[:], rmT_d.ap(), [], ["rm16"])
        dma("pool", id16[:], ident_d.ap(), [], ["id16"])
        dma("sp", condS[:], condT_d.ap(), [], ["condS"])
        dma("sp", gfT[:], gfT_d.ap(), [], ["gfT"])
        dma("sp", qkg[:], qkg_d.ap(), [], ["qkg"])
        P.op("dve", lambda e: e.memset(ones16[:], 1.0), reads=[], writes=["ones16"])
        act(scT[:], condS[:], AF.Silu, ["condS"], ["scT"])

        def adaln_gen(l):
            par = l % 2
            dma("sp", badaT[:], badaT_d[l], [], ["badaT"])
            dma("sp", g1T[:], g1T_d[l], [], ["g1T"])
            dma("sp", g2T[:], g2T_d[l], [], ["g2T"])
            dma("sp", convT[:], convT_d[l], [], ["convT"])
            b = psa.alloc()
            for j in range(96):
                w3, wk = wload(w_ada_d[l][:, j * 128:(j + 1) * 128], 16, 128, page=(None if l == 0 else 36 + 4 * (j % 3)))
                pairs = [(w3[:, kc, :], scT[:, 2 * kc:2 * kc + 2]) for kc in range(16)]
                mm_group(banks[b][:, 2 * j:2 * j + 2], pairs, wk + ["scT"], PSK(b))
                yield
            m3 = modT[:].rearrange("p (j c) -> p j c", c=2)
            b3 = banks[b][:, 0:192].rearrange("p (j c) -> p j c", c=2)
            for c in range(2):
                dve_tt(m3[:, :, c], b3[:, :, c], badaT[:], ALU.add, PSK(b) + ["badaT"], ["modT"])
            psa.release(b)
            for c in range(2):
                def mcol(which, c=c):
                    return m3[:, which * 16:(which + 1) * 16, c]
                A1 = AB[:, (0 * 2 + c) * 16:(0 * 2 + c) * 16 + 16]
                B1 = AB[:, (1 * 2 + c) * 16:(1 * 2 + c) * 16 + 16]
                A2 = AB[:, (2 * 2 + c) * 16:(2 * 2 + c) * 16 + 16]
                B2 = AB[:, (3 * 2 + c) * 16:(3 * 2 + c) * 16 + 16]
                G1 = GT[:, par * 64 + (0 * 2 + c) * 16:par * 64 + (0 * 2 + c) * 16 + 16]
                G2 = GT[:, par * 64 + (1 * 2 + c) * 16:par * 64 + (1 * 2 + c) * 16 + 16]
                dve_stt(A1, mcol(1), 1.0, g1T[:], ALU.add, ALU.mult, ["modT", "g1T"], ["AB"])
                dve_stt(A2, mcol(4), 1.0, g2T[:], ALU.add, ALU.mult, ["modT", "g2T"], ["AB"])
                dve_copy(B1, mcol(0), ["modT"], ["AB"])
                dve_copy(B2, mcol(3), ["modT"], ["AB"])
                dve_copy(G1, mcol(2), ["modT"], [("GT", par)])
                dve_copy(G2, mcol(5), ["modT"], [("GT", par)])

        def ABcol(which, c, e):
            o = (which * 2 + c) * 16 + e
            return AB[:, o:o + 1]

        def GTcol(par, which, c, e):
            o = par * 64 + (which * 2 + c) * 16 + e
            return GT[:, o:o + 1]

        def drain(gen, n=None):
            if gen is None:
                return
            k = 0
            while n is None or k < n:
                try:
                    next(gen)
                except StopIteration:
                    return
                k += 1

        def xs_ap(slot, arena="WS"):
            return ARENA[arena][:, slot * 16 * 512:(slot + 1) * 16 * 512].bitcast(F32).rearrange("p (e c) -> p e c", e=16)

        def norm_phase(mode, which=0):
            drain(norm_gen(mode, which))

        def norm_gen(mode, which=0, xs_arena="WS", intile=False):
            SQk = [("WS", 48), ("WS", 49)]
            SQ = [bfp("WS", 48)[:, 0:256], bfp("WS", 49)[:, 0:256]]
            deep = (mode == "mid" and xs_arena == "WS")
            if deep:
                SQk += [("R2", 40 + i) for i in range(4)]
                SQ += [bfp("R2", 40 + i)[:, 0:256] for i in range(4)]
            NSQ = len(SQ)
            RKs = [[("WS", 50)], [("WS", 51)]]
            Rrs = [f32p("WS", 50, 1), f32p("WS", 51, 1)]
            TMPk = [("R2", 48), ("R2", 49), ("R2", 50), ("R2", 51)]
            TMP = [f32p("R2", 48, 1), f32p("R2", 49, 1), f32p("R2", 50, 1), f32p("R2", 51, 1)]
            if deep:
                TMPk += [("R2", 32 + i) for i in range(4)]
                TMP += [f32p("R2", 32 + i, 1) for i in range(4)]
            NTMP = len(TMP)
            NH2 = 2 * NT

            def stage1(ht):
                t, hh = divmod(ht, 2)
                slot = ht % 3
                XS = xs_ap(slot, xs_arena)
                xk = pg(xs_arena, slot * 16, 16)
                xTk = [("xT", t, e) for e in range(16)]
                RK = RKs[ht % 2]
                Rr = Rrs[ht % 2]
                if mode == "in":
                    for cc in range(2):
                        chunk = 2 * ht + cc
                        rs = (chunk % 6)
                        XT = f32p("R2", rs * 8, 8)
                        dma("sp", XT, xin[chunk * 128:(chunk + 1) * 128, :], [], pg("R2", rs * 8, 8))
                    for e in range(16):
                        b = psa.alloc()
                        for cc in range(2):
                            chunk = 2 * ht + cc
                            rs = chunk % 6
                            XT = f32p("R2", rs * 8, 8)
                            pe_tr(banks[b][:, cc * 128:(cc + 1) * 128], XT[:, e * 128:(e + 1) * 128], id32[:],
                                  pg("R2", rs * 8, 8) + ["id32"], PSK(b))
                        copy_any(e, XS[:, e, :], banks[b][:, 0:256], PSK(b), xk)
                        psa.release(b)
                    dma("sp", xT[t][:, :, hh * 256:(hh + 1) * 256], XS, xk, xTk)
                else:
                    dma("sp", XS, xT[t][:, :, hh * 256:(hh + 1) * 256], xTk, xk)
                sb_ = psa.alloc()
                for e in range(16):
                    act(SQ[e % NSQ], XS[:, e, :], AF.Square, xk, [SQk[e % NSQ]])
                    pe_mm(banks[sb_][:, 0:256], ones16[:], SQ[e % NSQ], (e == 0), (e == 15), [SQk[e % NSQ], "ones16"], PSK(sb_))
                act(Rr, banks[sb_][:, 0:256], AF.Sqrt, PSK(sb_), RK, bias=EPS, scale=1.0 / D)
                psa.release(sb_)
                dve_recip(Rr, Rr, RK, RK)

            def stage2(ht):
                t, hh = divmod(ht, 2)
                c = cond_of(t)
                slot = ht % 3
                XS = xs_ap(slot, xs_arena)
                xk = pg(xs_arena, slot * 16, 16)
                RK = RKs[ht % 2]
                Rr = Rrs[ht % 2]
                if mode != "out":
                    for e in range(16):
                        A = ABcol(2 * which, c, e)
                        B = ABcol(2 * which + 1, c, e)
                        tm = TMP[e % NTMP]
                        dve_stt(tm, XS[:, e, :], A, Rr, ALU.mult, ALU.mult, xk + ["AB"] + RK, [TMPk[e % NTMP]])
                        dst = Hap(t, e)[:, hh * 256:(hh + 1) * 256]
                        if e % 4 == 3:
                            dve_ts(dst, tm, B, None, ALU.add, ALU.bypass, [TMPk[e % NTMP], "AB"], pg("R1", Hpage(t, e)))
                        else:
                            act(dst, tm, AF.Identity, [TMPk[e % NTMP], "AB"], pg("R1", Hpage(t, e)), bias=B, scale=1.0)
                else:
                    for e in range(16):
                        dve_stt(XS[:, e, :], XS[:, e, :], gfT[:, e:e + 1], Rr, ALU.mult, ALU.mult, xk + ["gfT"] + RK, xk)
                    for cc in range(2):
                        chunk = 2 * ht + cc
                        rs = chunk % 6
                        OT = f32p("R2", rs * 8, 8)
                        ok = pg("R2", rs * 8, 8)
                        for e4 in range(4):
                            b = psa.alloc()
                            for q in range(4):
                                e = e4 * 4 + q
                                pe_tr(banks[b][:, q * 128:(q + 1) * 128], XS[:, e, cc * 128:(cc + 1) * 128], id32[:], xk + ["id32"], PSK(b))
                            copy_any(e4, OT[:, e4 * 512:(e4 + 1) * 512], banks[b][:], PSK(b), ok)
                            psa.release(b)
                        dma("sp", yout[chunk * 128:(chunk + 1) * 128, :], OT, ok, [("yout", chunk)])

            if intile:
                for t_ in range(NT):
                    stage1(2 * t_)
                    stage1(2 * t_ + 1)
                    stage2(2 * t_)
                    stage2(2 * t_ + 1)
                    yield
                return
            stage1(0)
            for ht in range(NH2):
                if ht + 1 < NH2:
                    stage1(ht + 1)
                stage2(ht)
                if ht % 2 == 1:
                    yield

        def qk_norm(pb, t, gcol, st, out_bf, out_keys, i, out32=None, out32_keys=None):
            rope = t >= 2
            sqk, sq = st["sq"][i % 2]
            rk, rr = st["r"][i % 2]
            act(sq, banks[pb][:], AF.Square, PSK(pb), sqk)
            sb_ = psa.alloc()
            pe_mm(banks[sb_][:], ones16[:], sq, True, True, sqk + ["ones16"], PSK(sb_))
            act(rr, banks[sb_][:], AF.Sqrt, PSK(sb_), rk, bias=EPS, scale=1.0 / HD)
            psa.release(sb_)
            dve_recip(rr, rr, rk, rk)
            if not rope:
                if out32 is None:
                    dve_stt(out_bf, banks[pb][:], gcol, rr, ALU.mult, ALU.mult, PSK(pb) + ["qkg"] + rk, out_keys)
                else:
                    dve_stt(out32, banks[pb][:], gcol, rr, ALU.mult, ALU.mult, PSK(pb) + ["qkg"] + rk, out32_keys)
                    act(out_bf, out32, AF.Copy, out32_keys, out_keys)
                psa.release(pb)
                return
            q32k, q32 = st["qn"][i % 2]
            qnk, qn = st["qn16"][i % 2]
            t1k, t1 = st["t1"][0]
            ck_, cs_ = get_rope(t)
            dve_stt(qn, banks[pb][:], gcol, rr, ALU.mult, ALU.mult, PSK(pb) + ["qkg"] + rk, qnk)
            psa.release(pb)
            rb = psa.alloc()
            pe_mm(banks[rb][:], rm16[:], qn, True, True, qnk + ["rm16"], PSK(rb))
            dve_tt(t1, banks[rb][:], cs_[1], ALU.mult, PSK(rb) + ck_, t1k)
            psa.release(rb)
            pool_tt(q32, qn, cs_[0], ALU.mult, qnk + ck_, q32k)
            pool_tt(out_bf, q32, t1, ALU.add, q32k + t1k, out_keys)

        def qk_pipeline(units, st):
            n = len(units)
            S = [None] * n

            def stageA(u):
                un = units[u]
                pb = psa.alloc()
                S[u] = pb
                proj_group(un["w3"], un["wk"], un["t"], pb)
                sqk, sq = st["sq"][u % 3]
                act(sq, banks[pb][:], AF.Square, PSK(pb), sqk)

            def stageB(u):
                un = units[u]
                pb = S[u]
                t = un["t"]
                sqk, sq = st["sq"][u % 3]
                rk, rr = st["r"][u % 3]
                sb_ = psa.alloc()
                pe_mm(banks[sb_][:], ones16[:], sq, True, True, sqk + ["ones16"], PSK(sb_))
                act(rr, banks[sb_][:], AF.Sqrt, PSK(sb_), rk, bias=EPS, scale=1.0 / HD)
                psa.release(sb_)
                dve_recip(rr, rr, rk, rk)
                if t >= 2:
                    qnk, qn = st["qn16"][u % 3]
                    dve_stt(qn, banks[pb][:], un["gcol"], rr, ALU.mult, ALU.mult, PSK(pb) + ["qkg"] + rk, qnk)
                elif un["nk"] is None:
                    dve_stt(un["out_bf"], banks[pb][:], un["gcol"], rr, ALU.mult, ALU.mult, PSK(pb) + ["qkg"] + rk, un["out_keys"])
                else:
                    o32k, o32 = st["q32"][u % 2]
                    dve_stt(o32, banks[pb][:], un["gcol"], rr, ALU.mult, ALU.mult, PSK(pb) + ["qkg"] + rk, o32k)
                    act(un["out_bf"], o32, AF.Copy, o32k, un["out_keys"])
                psa.release(pb)

            def stageC(u):
                un = units[u]
                t = un["t"]
                if t >= 2:
                    qnk, qn = st["qn16"][u % 3]
                    t1k, t1 = st["t1"][u % 2]
                    q32k, q32 = st["q32"][u % 2]
                    ck_, cs_ = get_rope(t)
                    rb = psa.alloc()
                    pe_mm(banks[rb][:], rm16[:], qn, True, True, qnk + ["rm16"], PSK(rb))
                    act(t1, banks[rb][:], AF.Copy, PSK(rb), t1k)
                    psa.release(rb)
                    pool_tt(t1, t1, cs_[1], ALU.mult, t1k + ck_, t1k)
                    pool_tt(q32, qn, cs_[0], ALU.mult, qnk + ck_, q32k)
                    pool_tt(un["out_bf"], q32, t1, ALU.add, q32k + t1k, un["out_keys"])
                elif un["nk"] is not None:
                    l_, kvh = un["nk"]
                    o32k, o32 = st["q32"][u % 2]
                    tb = psa.alloc()
                    for c4 in range(4):
                        pe_tr(banks[tb][:, c4 * 128:(c4 + 1) * 128], o32[:, c4 * 128:(c4 + 1) * 128], id32[:], o32k + ["id32"], PSK(tb))
                    nkk, nks = st["t1"][1]
                    copy_any(1, nks, banks[tb][:], PSK(tb), nkk)
                    psa.release(tb)
                    for s2 in range(2):
                        dst = nk_d[2 * t + s2, l_, :, kvh * 128:(kvh + 1) * 128].rearrange("(h p) d -> p h d", p=128)
                        dma("sp", dst, nks[:, s2 * 256:(s2 + 1) * 256].rearrange("p (h d) -> p h d", h=2), nkk, [("nk", l_, t, kvh, s2)])

            for i in range(n + 2):
                if i < n:
                    stageA(i)
                if 0 <= i - 1 < n:
                    stageB(i - 1)
                if 0 <= i - 2 < n:
                    stageC(i - 2)

        rope_u = {"n": 0}

        def get_rope(t):
            sl = rope_u["n"] % 2
            rope_u["n"] += 1
            p = 44 + 4 * sl
            Cc = f32p("WS", p, 2)
            Ss = f32p("WS", p + 2, 2)
            dma("sp", Cc, ropeC_d[:, (t - 2) * 512:(t - 1) * 512], [], pg("WS", p, 2))
            dma("sp", Ss, ropeS_d[:, (t - 2) * 512:(t - 1) * 512], [], pg("WS", p + 2, 2))
            return pg("WS", p, 4), (Cc, Ss)

        def KT(kvh, tok0, n):
            o = kvh * KTOK + tok0
            return R2[:, o:o + n]

        def KTkeys(kvh, tok0, n):
            o = kvh * KTOK + tok0
            return pg("R2", o // 512, (o + n - 1) // 512 - o // 512 + 1)

        def Vap(c, kvh):
            return R2[:, (26 + c) * 512 + kvh * 128:(26 + c) * 512 + (kvh + 1) * 128]

        def tile_tok0(t):
            return t * 512

        def proj_group(w3, wk, t, pb):
            pairs = [(w3[:, kc, :], Hap(t, kc)) for kc in range(16)]
            mm_group(banks[pb][:], pairs, wk + pg("R1", t * 16, 16), PSK(pb))

        def phase2(l):
            wl = w_in_d[l]
            wstate["limit"] = 18
            wstate["pos"] = 0
            st = {
                "sq": [(pg("WS", 24 + i), bfp("WS", 24 + i)) for i in range(3)],
                "qn16": [(pg("WS", 27 + i), bfp("WS", 27 + i)) for i in range(3)],
                "r": [(pg("WS", 30 + 2 * i, 2), f32p("WS", 30 + 2 * i)) for i in range(3)],
                "q32": [(pg("WS", 36 + 2 * i, 2), f32p("WS", 36 + 2 * i)) for i in range(2)],
                "qn": [(pg("WS", 36 + 2 * i, 2), f32p("WS", 36 + 2 * i)) for i in range(2)],
                "t1": [(pg("WS", 40 + 2 * i, 2), f32p("WS", 40 + 2 * i)) for i in range(2)],
                "o": [(pg("WS", 42), bfp("WS", 42)), (pg("WS", 43), bfp("WS", 43))],
            }
            ckp = walloc(2)
            CK16 = WS[:, ckp * 512:(ckp + 2) * 512].rearrange("p (h c) -> p h c", h=2)
            dma("pool", CK16, ck_d[l].rearrange("(h p) c -> p h c", p=128), [], pg("WS", ckp, 2))
            Vc = R2[:, (26 + 24) * 512:(26 + 26) * 512].rearrange("p (h c) -> p h c", h=2)
            dma("pool", Vc, cv_d[l].rearrange("(h p) c -> p h c", p=128), [], pg("R2", 50, 2))
            b = psa.alloc()
            b16 = banks[b][:].bitcast(BF16)
            for kvh in range(KVH):
                for h2 in range(2):
                    o = (kvh * 2 + h2) * 128
                    pe_tr(b16[:, o:o + 128], CK16[:, h2, kvh * 128:(kvh + 1) * 128], id16[:], pg("WS", ckp, 2) + ["id16"], PSK(b))
            for kvh in range(KVH):
                dve_copy(KT(kvh, NTOK, PAST), b16[:, kvh * 256:(kvh + 1) * 256], PSK(b), KTkeys(kvh, NTOK, PAST))
            psa.release(b)
            if DBG.get("ph2", 99) < 1:
                return
            kg = qkg[:, 2 * l + 1:2 * l + 2]
            qg = qkg[:, 2 * l:2 * l + 1]
            units = []
            for kvh in range(KVH):
                w3, wk = wload(wl[:, C_K + kvh * 128:C_K + (kvh + 1) * 128], 16, 128)
                for t in range(NT):
                    units.append(dict(w3=w3, wk=wk, t=t, gcol=kg, out_bf=KT(kvh, t * 512, 512), out_keys=KTkeys(kvh, t * 512, 512),
                                      nk=(l, kvh) if t < 2 else None))
            qk_pipeline(units, st)
            if DBG.get("ph2", 99) < 2:
                return
            w3, wk = wload(wl[:, C_V:C_V + 512], 16, 512)
            vdbg = DBG.get("v", 99)
            for c in range(24):
                if vdbg < 1:
                    break
                t, c4 = divmod(c, 4)
                pb = psa.alloc()
                pairs = [(Hap(t, kc)[:, c4 * 128:(c4 + 1) * 128], w3[:, kc, :]) for kc in range(16)]
                mm_group(banks[pb][:], pairs, wk + pg("R1", t * 16, 16), PSK(pb))
                copy_any(c, bfp("R2", 26 + c), banks[pb][:], PSK(pb), pg("R2", 26 + c))
                if t < 2 and vdbg >= 2:
                    vk, v32 = st["qn"][c % 2]
                    copy_any(c + 1, v32, banks[pb][:], PSK(pb), vk)
                    seq, pos0 = divmod(c * 128, PSEQ)
                    if vdbg >= 3:
                        dma("sp", nv_d[seq, l, pos0:pos0 + 128, :], v32, vk, [("nv", l, c)])
                psa.release(pb)
            if DBG.get("ph2", 99) < 3:
                return
            scale = 1.0 / math.sqrt(HD)
            for h in range(NH):
                g = h // 4
                w3, wk = wload(wl[:, C_Q + h * 128:C_Q + (h + 1) * 128], 16, 128)
                units = [dict(w3=w3, wk=wk, t=t, gcol=qg, out_bf=bfp("WS", 18 + t), out_keys=pg("WS", 18 + t), nk=None) for t in range(NT)]
                qk_pipeline(units, st)
                attention_head(l, h, g, st, scale)

        def attn_evacuate(h, t, ob, db, st):
            o32k, o32 = st["qn"][t % 2]
            dk, d32 = st["r"][t % 2]
            dve_copy(o32, banks[ob][:], PSK(ob), o32k)
            psa.release(ob)
            dve_copy(d32, banks[db][:], PSK(db), dk)
            psa.release(db)
            dve_recip(d32, d32, dk, dk)
            osk, osb = st["o"][t % 2]
            dve_tt(osb, o32, d32, ALU.mult, o32k + dk, osk)
            dma("sp", Osp[t][:, h, :], osb, osk, [("O", t, h)])

        def pe_batch(items):
            specs = [it[0] for it in items]
            rd, wr = [], []
            for it in items:
                rd += it[1]
                wr += it[2]
            pe_multi(specs, rd, wr)

        def attention_head(l, h, g, st, scale):
            PTk = [pg("WS", 24 + i) for i in range(6)]
            PT = [bfp("WS", 24 + i) for i in range(6)]
            for t in range(2):
                Q = bfp("WS", 18 + t)
                Qkeys = pg("WS", 18 + t)
                ob = psa.alloc()
                db = psa.alloc()
                sbs = {}
                for j in range(2):
                    s_ = psa.alloc()
                    sbs[j] = s_
                    specs, rk = [], []
                    for a_ in range(2):
                        tok0 = (2 * t + a_) * PSEQ + j * 128
                        specs.append((banks[s_][:, a_ * 256:(a_ + 1) * 256], KT(g, tok0, 128), Q[:, a_ * 256:(a_ + 1) * 256], (a_ == 0), True))
                        rk += KTkeys(g, tok0, 128)
                    pe_multi(specs, rk + Qkeys, PSK(s_))
                for j in range(2):
                    s_ = sbs.pop(j)
                    pt, ptk = PT[j], PTk[j]
                    act(pt, banks[s_][:], AF.Exp, PSK(s_), ptk, scale=scale)
                    psa.release(s_)
                    specs, rk = [], []
                    for a_ in range(2):
                        vc = (2 * t + a_) * 2 + j
                        specs.append((banks[ob][:, a_ * 256:(a_ + 1) * 256], Vap(vc, g), pt[:, a_ * 256:(a_ + 1) * 256], (j == 0 and a_ == 0), (j == 1)))
                        rk += pg("R2", 26 + vc)
                    specs.append((banks[db][:], ones16[:], pt, (j == 0), (j == 1)))
                    pe_multi(specs, rk + ptk + ["ones16"], PSK(ob) + PSK(db))
                attn_evacuate(h, t, ob, db, st)
            NP = 9
            stream = [(t, p) for t in range(2, NT) for p in range(NP)]
            sbs = {}

            def S_item(t, j):
                s_ = psa.alloc()
                sbs[(t, j)] = s_
                tok0 = 1024 + j * 128
                return ((banks[s_][:], KT(g, tok0, 128), bfp("WS", 18 + t), True, True), KTkeys(g, tok0, 128) + pg("WS", 18 + t), PSK(s_))
            pe_batch([S_item(t, j) for (t, p) in stream[:2] for j in (2 * p, 2 * p + 1)])
            obdb = {}
            for idx, (t, p) in enumerate(stream):
                if t not in obdb:
                    obdb[t] = (psa.alloc(), psa.alloc())
                if p == NP - 3 and t + 1 < NT:
                    obdb[t + 1] = (psa.alloc(), psa.alloc())
                ob, db = obdb[t]
                items = []
                for jj, j in enumerate((2 * p, 2 * p + 1)):
                    s_ = sbs.pop((t, j))
                    k = (2 * idx + jj) % 6
                    pt, ptk = PT[k], PTk[k]
                    act(pt, banks[s_][:], AF.Exp, PSK(s_), ptk, scale=scale)
                    psa.release(s_)
                    vc = 8 + j
                    items.append(((banks[ob][:], Vap(vc, g), pt, (j == 0), (j == 17)), pg("R2", 26 + vc) + ptk, PSK(ob)))
                    items.append(((banks[db][:], ones16[:], pt, (j == 0), (j == 17)), ptk + ["ones16"], PSK(db)))
                if idx + 2 < len(stream):
                    t2, p2 = stream[idx + 2]
                    pe_batch([S_item(t2, 2 * p2), S_item(t2, 2 * p2 + 1)])
                pe_batch(items)
                if p == NP - 1:
                    attn_evacuate(h, t, ob, db, st)

        def phase2_conv_gates(l):
            wl = w_in_d[l]
            wstate["limit"] = WRING
            wstate["pos"] = 0
            CB = [(pg("WS", 24 + 2 * i, 2), f32p("WS", 24 + 2 * i)) for i in range(3)]
            PR = [(pg("WS", 30 + 2 * i, 2), f32p("WS", 30 + 2 * i)) for i in range(3)]
            CS = [(pg("WS", 36 + 2 * i, 2), f32p("WS", 36 + 2 * i)) for i in range(2)]
            ACk, AC = pg("WS", 40, 2), f32p("WS", 40)
            u = 0
            for j in range(8):
                def cw(k, j=j):
                    return convT[:, j * 4 + k:j * 4 + k + 1]
                wpage = walloc(12)
                wb3, wbk = wload(wl[:, C_CB + j * 128:C_CB + (j + 1) * 128], 16, 128, page=wpage)
                wc3, wck = wload(wl[:, C_CC + j * 128:C_CC + (j + 1) * 128], 16, 128, page=wpage + 4)
                wx3, wxk = wload(wl[:, C_CX + j * 128:C_CX + (j + 1) * 128], 16, 128, page=wpage + 8)

                def conv_tile(t, j=j, cw=cw):
                    cbk, cb = CB[t % 3]
                    prk, pr = PR[t % 3]
                    dve_ts(AC, pr, cw(1), cw(3), ALU.mult, ALU.add, prk + ["convT"], ACk)
                    segs = [(0, 256), (256, 512)] if t < 2 else [(0, 512)]
                    for (a, b_) in segs:
                        dve_stt(AC[:, a + 1:b_], pr[:, a:b_ - 1], cw(0), AC[:, a + 1:b_], ALU.mult, ALU.add, prk + ACk + ["convT"], ACk)
                        dve_stt(AC[:, a:b_ - 1], pr[:, a + 1:b_], cw(2), AC[:, a:b_ - 1], ALU.mult, ALU.add, prk + ACk + ["convT"], ACk)
                    if t > 2:
                        pk2, pr2 = PR[(t - 1) % 3]
                        dve_stt(AC[:, 0:1], pr2[:, 511:512], cw(0), AC[:, 0:1], ALU.mult, ALU.add, pk2 + ACk + ["convT"], ACk)
                    if 2 <= t < NT - 1:
                        pk2, pr2 = PR[(t + 1) % 3]
                        dve_stt(AC[:, 511:512], pr2[:, 0:1], cw(2), AC[:, 511:512], ALU.mult, ALU.add, pk2 + ACk + ["convT"], ACk)
                    dve_tt(bfp("R2", j * 6 + t), cb, AC, ALU.mult, cbk + ACk, pg("R2", j * 6 + t))

                for t in range(NT):
                    cbk, cb = CB[t % 3]
                    prk, pr = PR[t % 3]
                    csk, cs = CS[u % 2]
                    pb = psa.alloc()
                    proj_group(wb3, wbk, t, pb)
                    act(cb, banks[pb][:], AF.Copy, PSK(pb), cbk)
                    psa.release(pb)
                    pb = psa.alloc()
                    proj_group(wc3, wck, t, pb)
                    act(cs, banks[pb][:], AF.Copy, PSK(pb), csk)
                    psa.release(pb)
                    pb = psa.alloc()
                    proj_group(wx3, wxk, t, pb)
                    dve_tt(pr, banks[pb][:], cs, ALU.mult, PSK(pb) + csk, prk)
                    psa.release(pb)
                    u += 1
                    if t >= 1:
                        conv_tile(t - 1)
                conv_tile(NT - 1)
            SGk = [pg("WS", 24 + i) for i in range(4)]
            SG = [bfp("WS", 24 + i) for i in range(4)]
            u = 0
            for e2 in range(32):
                which, e = divmod(e2, 16)
                w3, wk = wload(wl[:, C_GA + e2 * 128:C_GA + (e2 + 1) * 128], 16, 128)
                for t in range(NT):
                    pb = psa.alloc()
                    proj_group(w3, wk, t, pb)
                    act(SG[u % 4], banks[pb][:], AF.Sigmoid, PSK(pb), SGk[u % 4])
                    psa.release(pb)
                    dma("sp", Gsp[t][:, 2 * e + which, :], SG[u % 4], SGk[u % 4], [("G", t, 2 * e + which)])
                    u += 1

        def load_R1(src, nchunk, t_list, keyname):
            for t in t_list:
                tt = t_list.index(t)
                nparts = 4 if nchunk > 16 else 2
                for half in range(nparts):
                    n2 = nchunk // nparts
                    c0 = half * n2
                    dst = R1[:, (tt * nchunk + c0) * 512:(tt * nchunk + c0 + n2) * 512].rearrange("p (c k) -> p c k", c=n2)
                    dma("sp", dst, src[t][:, c0:c0 + n2, :], [(keyname, t, c) for c in range(c0, c0 + n2)], pg("R1", tt * nchunk + c0, n2))

        def phase3(l):
            load_R1(Osp, 16, list(range(NT)), "O")
            GLk = [pg("WS", 24 + 2 * i, 2) for i in range(3)]
            GL = [bfp("WS", 24 + 2 * i, 2).rearrange("p (w c) -> p w c", w=2) for i in range(3)]
            T1 = [(pg("WS", 30 + 2 * i, 2), f32p("WS", 30 + 2 * i)) for i in range(2)]
            T2 = [(pg("WS", 34 + 2 * i, 2), f32p("WS", 34 + 2 * i)) for i in range(2)]
            MOk = [pg("WS", 38 + i) for i in range(3)]
            MO = [bfp("WS", 38 + i) for i in range(3)]
            units = [(e, t) for e in range(16) for t in range(NT)]
            PF = 2

            def gload(i):
                e, t = units[i]
                dma("sp", GL[i % 3], Gsp[t][:, 2 * e:2 * e + 2, :], [("G", t, 2 * e), ("G", t, 2 * e + 1)], GLk[i % 3])
            for i in range(PF):
                gload(i)
            wa3 = wb3 = None
            for i, (e, t) in enumerate(units):
                if t == 0:
                    wp = walloc(6)
                    wa3, wak = wload(w_a_d[l][:, e * 128:(e + 1) * 128], 8, 128, page=wp)
                    wb3, wbk = wload(w_b_d[l][:, e * 128:(e + 1) * 128], 16, 128, page=wp + 2)
                if i + PF < len(units):
                    gload(i + PF)
                ua = psa.alloc()
                pairs = [(wa3[:, kc, :], bfp("R2", kc * 6 + t)) for kc in range(8)]
                mm_group(banks[ua][:], pairs, wak + [("R2", kc * 6 + t) for kc in range(8)], PSK(ua))
                ub = psa.alloc()
                pairs = [(wb3[:, kc, :], Hap(t, kc)) for kc in range(16)]
                mm_group(banks[ub][:], pairs, wbk + pg("R1", t * 16, 16), PSK(ub))
                t1k, t1 = T1[i % 2]
                t2k, t2 = T2[i % 2]
                gl = GL[i % 3]
                dve_tt(t1, banks[ua][:], gl[:, 0, :], ALU.mult, PSK(ua) + GLk[i % 3], t1k)
                psa.release(ua)
                dve_tt(t2, banks[ub][:], gl[:, 1, :], ALU.mult, PSK(ub) + GLk[i % 3], t2k)
                psa.release(ub)
                mo = MO[i % 3]
                dve_tt(mo, t1, t2, ALU.add, t1k + t2k, MOk[i % 3])
                dma("sp", Msp[t][:, e, :], mo, MOk[i % 3], [("M", t, e)])

        def resid_phase(l, wsrc, kc_n, which_gt, t_list, rslot, e_list=None, tile_major=False, after_tile=None):
            XL = [(pg("WS", 24 + 2 * i, 2), f32p("WS", 24 + 2 * i)) for i in range(4)]
            if e_list is None:
                e_list = list(range(16))
            if tile_major:
                units = [(e, t) for t in t_list for e in e_list]
            else:
                units = [(e, t) for e in e_list for t in t_list]
            PF = 2

            def xload(i):
                e, t = units[i]
                dma("sp", XL[i % 4][1], xT[t][:, e, :], [("xT", t, e)], XL[i % 4][0])
            for i in range(min(PF, len(units))):
                xload(i)
            wcache = {}
            for i, (e, t) in enumerate(units):
                if e not in wcache:
                    wcache[e] = wload(wsrc[:, e * 128:(e + 1) * 128], kc_n, 128)
                w3, wk = wcache[e]
                if i + PF < len(units):
                    xload(i + PF)
                tt = t_list.index(t)
                pb = psa.alloc()
                nsub = 4 if kc_n > 16 else 1
                step = kc_n // nsub
                for sg in range(nsub):
                    k0, k1 = sg * step, (sg + 1) * step
                    pairs = [(w3[:, kc, :], bfp("R1", tt * kc_n + kc)) for kc in range(k0, k1)]
                    mm_group(banks[pb][:], pairs, wk + pg("R1", tt * kc_n + k0, k1 - k0), PSK(pb), first=(sg == 0), last=(sg == nsub - 1))
                xk, xs = XL[i % 4]
                gcol = GTcol(l % 2, which_gt, cond_of(t), e)
                dve_stt(xs, banks[pb][:], gcol, xs, ALU.mult, ALU.add, PSK(pb) + xk + [("GT", l % 2)], xk)
                psa.release(pb)
                dma("sp", xT[t][:, e, :], xs, xk, [("xT", t, e)])
                if tile_major and after_tile is not None and e == e_list[-1]:
                    after_tile(t)

        def phase4(l):
            load_R1(Msp, 16, list(range(NT)), "M")
            tl = list(range(NT))
            if not DBG.get("ovl4", 1):
                resid_phase(l, w_o_d[l], 16, 0, tl, 0)
                norm_phase("mid", which=1)
                return
            NTAIL = 4
            resid_phase(l, w_o_d[l], 16, 0, tl, 0, e_list=list(range(16 - NTAIL)))
            wstate["pos"] = 0
            ng = norm_gen("mid", 1, "R2", intile=True)
            resid_phase(l, w_o_d[l], 16, 0, tl, 0, e_list=list(range(16 - NTAIL, 16)), tile_major=True, after_tile=lambda t: next(ng))
            drain(ng)

        def phase6(l, bg=None, normgen=None):
            SIk = [pg("WS", 24 + 2 * i, 2) for i in range(3)]
            SI = [f32p("WS", 24 + 2 * i) for i in range(3)]
            AOk = [pg("WS", 30 + i) for i in range(4)]
            AO = [bfp("WS", 30 + i) for i in range(4)]
            ust = {"u": 0}
            blocks = {}

            def load(f):
                wp = walloc(8)
                wg3, wgk = wload(w_gate_d[l][:, f * 128:(f + 1) * 128], 16, 128, page=wp)
                wu3, wuk = wload(w_up_d[l][:, f * 128:(f + 1) * 128], 16, 128, page=wp + 4)
                blocks[f] = (wg3, wgk, wu3, wuk)

            def unit(f, t):
                wg3, wgk, wu3, wuk = blocks[f]
                u = ust["u"]
                gb = psa.alloc()
                proj_group(wg3, wgk, t, gb)
                ub = psa.alloc()
                proj_group(wu3, wuk, t, ub)
                si = SI[u % 3]
                act(si, banks[gb][:], AF.Silu, PSK(gb), SIk[u % 3])
                psa.release(gb)
                ao = AO[u % 4]
                dve_tt(ao, banks[ub][:], si, ALU.mult, PSK(ub) + SIk[u % 3], AOk[u % 4])
                psa.release(ub)
                dma("sp", Asp[t][:, f, :], ao, AOk[u % 4], [("A", t, f)])
                ust["u"] = u + 1

            f0 = 0
            if normgen is not None:
                wstate["pos"] = 0
                f0 = 3
                for f in range(f0):
                    load(f)
                for t in range(NT):
                    next(normgen)
                    for f in range(f0):
                        unit(f, t)
                drain(normgen)
            for f in range(f0, NF):
                load(f)
                for t in range(NT):
                    unit(f, t)
                drain(bg, 3)
            drain(bg)

        phases = []
        for l in range(depth):
            if l == 0:
                phases.append(lambda: drain(adaln_gen(0)))
            phases.append(lambda l=l: norm_phase("in" if l == 0 else "mid", which=0))
            phases.append(lambda l=l: phase2(l))
            phases.append(lambda l=l: phase2_conv_gates(l))
            phases.append(lambda l=l: phase3(l))
            phases.append(lambda l=l: phase4(l))
            phases.append(lambda l=l: phase6(l, adaln_gen(l + 1) if l + 1 < depth else None))

            def ph7(l=l):
                for third in range(3):
                    tl = [2 * third, 2 * third + 1]
                    load_R1(Asp, NF, tl, "A")
                    resid_phase(l, w_down_d[l], NF, 1, tl, 0)
            phases.append(ph7)
        phases.append(lambda: norm_phase("out"))
        for i, ph in enumerate(phases):
            if maxph is not None and i >= maxph and i != len(phases) - 1:
                continue
            ph()

        emit(nc, P, sems)
    return nc


def _rope_tables():
    rows = STOK // 64
    row = np.repeat(np.arange(rows, dtype=np.float32), 64)
    col = np.tile(np.arange(64, dtype=np.float32), rows)
    n_freq = HD // 4
    inv = (np.float32(10000.0) ** (-np.arange(n_freq, dtype=np.float32) / np.float32(n_freq))).astype(np.float32)
    ang = np.concatenate([row[:, None] * inv, col[:, None] * inv], axis=-1).astype(np.float32)
    cos = np.cos(ang).astype(np.float32)
    sin = np.sin(ang).astype(np.float32)
    C = np.ascontiguousarray(np.concatenate([cos, cos], axis=1).T)
    S = np.ascontiguousarray(np.concatenate([sin, sin], axis=1).T)
    return C, S


def _rot_matrix_T():
    Rm = np.zeros((128, 128), np.float32)
    for d in range(64):
        Rm[d, d + 64] = -1.0
        Rm[d + 64, d] = 1.0
    return np.ascontiguousarray(Rm.T)


def make_in_maps(inp, depth=DEPTH):
    f = lambda a: np.ascontiguousarray(np.asarray(a, dtype=np.float32))
    x_prompt, x_sample = f(inp["x_prompt"]), f(inp["x_sample"])
    cache_k, cache_v = f(inp["cache_k"]), f(inp["cache_v"])
    c, c_ctx = f(inp["c"]), f(inp["c_ctx"])
    C, S = _rope_tables()
    shared = {
        "w_ada": f(inp["w_ada"][:depth]), "w_in": f(inp["w_in"][:depth]), "w_a": f(inp["w_a"][:depth]), "w_b": f(inp["w_b"][:depth]),
        "w_o": f(inp["w_o"][:depth]), "w_gate": f(inp["w_gate"][:depth]), "w_up": f(inp["w_up"][:depth]), "w_down": f(inp["w_down"][:depth]),
        "badaT": np.ascontiguousarray(f(inp["b_ada"]).reshape(DEPTH, 96, 128).transpose(0, 2, 1)),
        "g1T": np.ascontiguousarray(f(inp["norm1"]).reshape(DEPTH, 16, 128).transpose(0, 2, 1)),
        "g2T": np.ascontiguousarray(f(inp["norm2"]).reshape(DEPTH, 16, 128).transpose(0, 2, 1)),
        "gfT": np.ascontiguousarray(f(inp["norm_f"]).reshape(16, 128).T),
        "ropeC": C, "ropeS": S,
        "ident": np.eye(128, dtype=np.float32),
        "rmT": _rot_matrix_T(),
    }
    qkg = np.zeros((128, 2 * DEPTH), np.float32)
    for l in range(DEPTH):
        qkg[:, 2 * l] = inp["q_gain"][l]
        qkg[:, 2 * l + 1] = inp["k_gain"][l]
    shared["qkg"] = qkg
    cw = f(inp["conv_w"]).reshape(DEPTH, 3, 8, 128)
    cb = f(inp["conv_b"]).reshape(DEPTH, 8, 128)
    convT = np.zeros((DEPTH, 128, 8, 4), np.float32)
    convT[:, :, :, 0:3] = cw.transpose(0, 3, 2, 1)
    convT[:, :, :, 3] = cb.transpose(0, 2, 1)
    shared["convT"] = np.ascontiguousarray(convT.reshape(DEPTH, 128, 32))
    maps = []
    for i in range(NCORE):
        m = dict(shared)
        m["xin"] = np.ascontiguousarray(np.concatenate(
            [x_prompt[NPSEQ * i:NPSEQ * (i + 1)].reshape(NPSEQ * PSEQ, D), x_sample[i]], axis=0))
        m["ck"] = np.ascontiguousarray(cache_k[i].reshape(DEPTH, PAST, 512))
        m["cv"] = np.ascontiguousarray(cache_v[i].reshape(DEPTH, PAST, 512))
        cond = np.stack([c_ctx, c[i]], axis=0)
        m["condT"] = np.ascontiguousarray(cond.reshape(2, 16, 128).transpose(2, 1, 0).reshape(128, 32))
        maps.append(m)
    return maps


_NC_CACHE = {}


def kernel(**inputs):
    if "nc" not in _NC_CACHE:
        _NC_CACHE["nc"] = build(DEPTH)
    nc = _NC_CACHE["nc"]
    maps = make_in_maps(inputs)
    res = run_bass_kernel_spmd(nc, maps, core_ids=list(range(NCORE)))
    B = NCORE * NPSEQ
    y_prompt = np.empty((B, PSEQ, D), np.float32)
    y_sample = np.empty((NCORE, STOK, D), np.float32)
    nk = np.empty((B, DEPTH, PSEQ, KVH, HD), np.float32)
    nv = np.empty((B, DEPTH, PSEQ, KVH, HD), np.float32)
    for i in range(NCORE):
        r = res.results[i]
        y = r["yout"]
        y_prompt[NPSEQ * i:NPSEQ * (i + 1)] = y[:NPSEQ * PSEQ].reshape(NPSEQ, PSEQ, D)
        y_sample[i] = y[NPSEQ * PSEQ:]
        nk[NPSEQ * i:NPSEQ * (i + 1)] = r["nk"].reshape(NPSEQ, DEPTH, PSEQ, KVH, HD)
        nv[NPSEQ * i:NPSEQ * (i + 1)] = r["nv"].reshape(NPSEQ, DEPTH, PSEQ, KVH, HD)
    return (y_prompt, y_sample, nk, nv)
```

```python
import math
from contextlib import ExitStack

import numpy as np
import concourse.bass as bass
import concourse.mybir as mybir
from concourse.bass_utils import run_bass_kernel_spmd

F32 = mybir.dt.float32
BF16 = mybir.dt.bfloat16
AF = mybir.ActivationFunctionType
ALU = mybir.AluOpType

D = 2048
DEPTH = 4
NCORE = 8
PSEQ = 256
NPSEQ = 4
STOK = 2048
PAST = 256
NTOK = NPSEQ * PSEQ + STOK
NT = NTOK // 512
KVH = 4
NH = 16
HD = 128
D_CONV = 1024
D_FF = 5632
NF = D_FF // 128
IN_COLS = 10240
EPS = 1e-6
C_Q, C_K, C_V, C_CB, C_CC, C_CX, C_GA, C_GB = 0, 2048, 2560, 3072, 4096, 5120, 6144, 8192
KTOK = NTOK + PAST

DBG = {}
ENGS = ("pe", "act", "dve", "pool", "sp")
NDMA = 12


class Plan:
    def __init__(self):
        self.ops = {e: [] for e in ENGS}
        self.count = {e: 0 for e in ENGS}
        self.waited = {e: {} for e in ENGS}
        self.last_w = {}
        self.reads = {}
        self.dq = {q: {"n": 0, "vals": [0] * NDMA} for q in ("sp", "pool")}

    def op(self, eng, fn, reads=(), writes=(), dma=False):
        deps = {}

        def add(tok):
            if tok is None:
                return
            k, v, e = tok
            if e == "pe" and eng == "pe":
                return
            if deps.get(k, 0) < v:
                deps[k] = v

        lw = self.last_w
        xr = [p for p in reads if isinstance(p, tuple) and p[0] == "ps"]
        if xr:
            reads = [p for p in reads if not (isinstance(p, tuple) and p[0] == "ps")]
            writes = list(writes) + [p for p in xr if p not in writes]
        for p in reads:
            add(lw.get(p))
        for p in writes:
            add(lw.get(p))
            rd = self.reads.get(p)
            if rd:
                for k, (v, e) in rd.items():
                    add((k, v, e))
        if dma:
            pool = self.dq[eng]
            idx = pool["n"] % NDMA
            pool["n"] += 1
            prev = pool["vals"][idx]
            if prev > 0:
                add((("dma", eng, idx), prev, None))
            pool["vals"][idx] = prev + 16
            mykey = ("dma", eng, idx)
            myval = prev + 16
            tok_e = None
        else:
            self.count[eng] += 1
            mykey = ("eng", eng)
            myval = self.count[eng]
            tok_e = eng
        waits = []
        wd = self.waited[eng]
        for k, v in deps.items():
            if wd.get(k, 0) >= v:
                continue
            wd[k] = v
            waits.append((k, v))
        self.ops[eng].append((waits, fn, mykey, dma))
        tok = (mykey, myval, tok_e)
        for p in writes:
            lw[p] = tok
            self.reads[p] = {}
        for p in reads:
            d = self.reads.get(p)
            if d is None:
                d = self.reads[p] = {}
            d[mykey] = (myval, tok_e)
        return tok

    def final_waits(self, eng):
        waits = []
        for e in ENGS:
            if self.count[e] > 0 and e != eng:
                waits.append((("eng", e), self.count[e]))
        for q, pool in self.dq.items():
            for idx, v in enumerate(pool["vals"]):
                if v > 0:
                    waits.append((("dma", q, idx), v))
        return waits


def all_semkeys():
    ks = [("eng", e) for e in ENGS]
    for q in ("sp", "pool"):
        for i in range(NDMA):
            ks.append(("dma", q, i))
    return ks


def emit(nc, plan, sems):
    fin = plan.final_waits("sp")
    with nc.Block() as block:
        def run(engname, e):
            for waits, fn, mykey, dma in plan.ops[engname]:
                for k, v in waits:
                    e.wait_ge(sems[k], v)
                ins = fn(e)
                ins.then_inc(sems[mykey], 16 if dma else 1)

        @block.tensor
        def _(e):
            run("pe", e)

        @block.scalar
        def _(e):
            run("act", e)

        @block.vector
        def _(e):
            run("dve", e)

        @block.gpsimd
        def _(e):
            run("pool", e)

        @block.sync
        def _(e):
            run("sp", e)
            for k, v in fin:
                e.wait_ge(sems[k], v)


R1P, R2P, WSP = 96, 52, 52
WRING = 24
STB = 24


def build(depth=DEPTH, maxph=None):
    nc = bass.Bass("TRN2", target_bir_lowering=False)

    def din(name, shape):
        return nc.dram_tensor(name, list(shape), F32, kind="ExternalInput")

    xin = din("xin", [NTOK, D])
    ck_d = din("ck", [DEPTH, PAST, 512])
    cv_d = din("cv", [DEPTH, PAST, 512])
    condT_d = din("condT", [128, 32])
    w_ada_d = din("w_ada", [depth, D, 6 * D])
    badaT_d = din("badaT", [DEPTH, 128, 96])
    g1T_d = din("g1T", [DEPTH, 128, 16])
    g2T_d = din("g2T", [DEPTH, 128, 16])
    gfT_d = din("gfT", [128, 16])
    qkg_d = din("qkg", [128, 2 * DEPTH])
    convT_d = din("convT", [DEPTH, 128, 32])
    w_in_d = din("w_in", [depth, D, IN_COLS])
    w_a_d = din("w_a", [depth, D_CONV, D])
    w_b_d = din("w_b", [depth, D, D])
    w_o_d = din("w_o", [depth, D, D])
    w_gate_d = din("w_gate", [depth, D, D_FF])
    w_up_d = din("w_up", [depth, D, D_FF])
    w_down_d = din("w_down", [depth, D_FF, D])
    ropeC_d = din("ropeC", [128, STOK])
    ropeS_d = din("ropeS", [128, STOK])
    ident_d = din("ident", [128, 128])
    rmT_d = din("rmT", [128, 128])

    yout = nc.dram_tensor("yout", [NTOK, D], F32, kind="ExternalOutput")
    nk_d = nc.dram_tensor("nk", [NPSEQ, DEPTH, PSEQ, 512], F32, kind="ExternalOutput")
    nv_d = nc.dram_tensor("nv", [NPSEQ, DEPTH, PSEQ, 512], F32, kind="ExternalOutput")

    xT = nc.dram_tensor("xT_s", [NT, 128, 16, 512], F32)
    Osp = nc.dram_tensor("O_s", [NT, 128, 16, 512], BF16)
    Gsp = nc.dram_tensor("G_s", [NT, 128, 32, 512], BF16)
    Msp = nc.dram_tensor("M_s", [NT, 128, 16, 512], BF16)
    Asp = nc.dram_tensor("A_s", [NT, 128, NF, 512], BF16)

    es = ExitStack()
    with es:
        def sb(name, shape, dt):
            return es.enter_context(nc.sbuf_tensor("sb_" + name, list(shape), dt))

        R1 = sb("R1", [128, R1P * 512], BF16)
        R2 = sb("R2", [128, R2P * 512], BF16)
        WS = sb("WS", [128, WSP * 512], BF16)
        ARENA = {"R1": R1, "R2": R2, "WS": WS}

        ones16 = sb("ones16", [128, 128], BF16)
        id32 = sb("id32", [128, 128], F32)
        id16 = sb("id16", [128, 128], BF16)
        rm16 = sb("rm16", [128, 128], BF16)
        condS = sb("condS", [128, 32], F32)
        scT = sb("scT", [128, 32], BF16)
        badaT = sb("badaT", [128, 96], F32)
        modT = sb("modT", [128, 192], F32)
        g1T = sb("g1T", [128, 16], F32)
        g2T = sb("g2T", [128, 16], F32)
        gfT = sb("gfT", [128, 16], F32)
        AB = sb("AB", [128, 4 * 2 * 16], F32)
        GT = sb("GT", [128, 2 * 2 * 2 * 16], F32)
        qkg = sb("qkg", [128, 2 * DEPTH], F32)
        convT = sb("convT", [128, 32], F32)

        banks = [es.enter_context(nc.psum_tensor(f"ps{i}", [128, 512], F32)) for i in range(8)]
        sems = {k: es.enter_context(nc.semaphore("s_" + "_".join(str(x) for x in k))) for k in all_semkeys()}

        P = Plan()

        def pg(arena, first, n=1):
            return [(arena, p) for p in range(first, first + n)]

        def bfp(arena, page, n=1):
            return ARENA[arena][:, page * 512:(page + n) * 512]

        def f32p(arena, page, n=2):
            return ARENA[arena][:, page * 512:(page + n) * 512].bitcast(F32)

        class PsumAlloc:
            def __init__(self):
                self.free = list(range(8))

            def alloc(self):
                assert self.free, "psum exhausted"
                return self.free.pop(0)

            def release(self, b):
                self.free.append(b)

        psa = PsumAlloc()

        def PSK(b):
            return [("ps", b)]

        wstate = {"pos": 0, "limit": WRING}

        def walloc(n):
            if wstate["pos"] + n > wstate["limit"]:
                wstate["pos"] = 0
            p = wstate["pos"]
            wstate["pos"] += n
            return p

        def dma(eng, out, in_, reads, writes):
            P.op(eng, lambda e: e.dma_start(out=out, in_=in_), reads=reads, writes=writes, dma=True)

        def wload(src2d, kc, ncols, page=None):
            npages = (kc * ncols + 511) // 512
            if page is None:
                page = walloc(npages)
            dst = WS[:, page * 512: page * 512 + kc * ncols].rearrange("p (k c) -> p k c", k=kc)
            keys = pg("WS", page, npages)
            dma("pool", dst, src2d.rearrange("(k p) c -> p k c", p=128), [], keys)
            return dst, keys

        def mm_group(bank_ap, pairs, reads, bank_keys, skip=False, first=True, last=True):
            n = len(pairs)

            def fn(e):
                ins = None
                for i, (l, r) in enumerate(pairs):
                    ins = e.matmul(bank_ap, l, r, start=(first and i == 0), stop=(last and i == n - 1))
                return ins
            P.op("pe", fn, reads=reads, writes=bank_keys)

        def act(out, in_, func, reads, writes, bias=0.0, scale=1.0):
            P.op("act", lambda e: e.activation(out=out, in_=in_, func=func, bias=bias, scale=scale), reads=reads, writes=writes)

        def dve_tt(out, in0, in1, op, reads, writes):
            P.op("dve", lambda e: e.tensor_tensor(out=out, in0=in0, in1=in1, op=op), reads=reads, writes=writes)

        def dve_stt(out, in0, scalar, in1, op0, op1, reads, writes):
            P.op("dve", lambda e: e.scalar_tensor_tensor(out=out, in0=in0, scalar=scalar, in1=in1, op0=op0, op1=op1), reads=reads, writes=writes)

        def dve_ts(out, in0, s1, s2, op0, op1, reads, writes):
            if s2 is None:
                P.op("dve", lambda e: e.tensor_scalar(out=out, in0=in0, scalar1=s1, scalar2=None, op0=op0), reads=reads, writes=writes)
            else:
                P.op("dve", lambda e: e.tensor_scalar(out=out, in0=in0, scalar1=s1, scalar2=s2, op0=op0, op1=op1), reads=reads, writes=writes)

        def dve_copy(out, in_, reads, writes):
            P.op("dve", lambda e: e.tensor_copy(out=out, in_=in_), reads=reads, writes=writes)

        def dve_recip(out, in_, reads, writes):
            P.op("dve", lambda e: e.reciprocal(out=out, in_=in_), reads=reads, writes=writes)

        def dve_recip2(x, scr, reads, writes):
            P.op("dve", lambda e: e.reciprocal_approx_accurate(out=x, in_=x, scratch=scr), reads=reads, writes=writes)

        def pool_tt(out, in0, in1, op, reads, writes):
            P.op("pool", lambda e: e.tensor_tensor(out=out, in0=in0, in1=in1, op=op), reads=reads, writes=writes)

        def pe_mm(out, lhsT, rhs, start, stop, reads, writes):
            P.op("pe", lambda e: e.matmul(out, lhsT, rhs, start=start, stop=stop), reads=reads, writes=writes)

        def pe_tr(out, in_, ident, reads, writes):
            P.op("pe", lambda e: e.transpose(out, in_, ident), reads=reads, writes=writes)

        def pe_multi(specs, reads, writes):
            def fn(e):
                ins = None
                for (o_, l_, r_, st_, sp_) in specs:
                    ins = e.matmul(o_, l_, r_, start=st_, stop=sp_, skip_group_check=True)
                return ins
            P.op("pe", fn, reads=reads, writes=writes)

        def copy_any(i, out, in_, reads, writes):
            if i % 2 == 0:
                act(out, in_, AF.Copy, reads, writes)
            else:
                dve_copy(out, in_, reads, writes)

        def Hpage(t, kc):
            return t * 16 + kc

        def Hap(t, kc):
            return bfp("R1", Hpage(t, kc))

        def cond_of(t):
            return 0 if t < 2 else 1

        dma("sp", id32[:], ident_d.ap(), [], ["id32"])
        dma("pool", rm16[:], rmT_d.ap(), [], ["rm16"])
        dma("pool", id16[:], ident_d.ap(), [], ["id16"])
        dma("sp", condS[:], condT_d.ap(), [], ["condS"])
        dma("sp", gfT[:], gfT_d.ap(), [], ["gfT"])
        dma("sp", qkg[:], qkg_d.ap(), [], ["qkg"])
        P.op("dve", lambda e: e.memset(ones16[:], 1.0), reads=[], writes=["ones16"])
        act(scT[:], condS[:], AF.Silu, ["condS"], ["scT"])

        def adaln_gen(l):
            par = l % 2
            dma("sp", badaT[:], badaT_d[l], [], ["badaT"])
            dma("sp", g1T[:], g1T_d[l], [], ["g1T"])
            dma("sp", g2T[:], g2T_d[l], [], ["g2T"])
            dma("sp", convT[:], convT_d[l], [], ["convT"])
            b = psa.alloc()
            for j in range(96):
                w3, wk = wload(w_ada_d[l][:, j * 128:(j + 1) * 128], 16, 128, page=(None if l == 0 else 36 + 4 * (j % 3)))
                pairs = [(w3[:, kc, :], scT[:, 2 * kc:2 * kc + 2]) for kc in range(16)]
                mm_group(banks[b][:, 2 * j:2 * j + 2], pairs, wk + ["scT"], PSK(b))
                yield
            m3 = modT[:].rearrange("p (j c) -> p j c", c=2)
            b3 = banks[b][:, 0:192].rearrange("p (j c) -> p j c", c=2)
            for c in range(2):
                dve_tt(m3[:, :, c], b3[:, :, c], badaT[:], ALU.add, PSK(b) + ["badaT"], ["modT"])
            psa.release(b)
            for c in range(2):
                def mcol(which, c=c):
                    return m3[:, which * 16:(which + 1) * 16, c]
                A1 = AB[:, (0 * 2 + c) * 16:(0 * 2 + c) * 16 + 16]
                B1 = AB[:, (1 * 2 + c) * 16:(1 * 2 + c) * 16 + 16]
                A2 = AB[:, (2 * 2 + c) * 16:(2 * 2 + c) * 16 + 16]
                B2 = AB[:, (3 * 2 + c) * 16:(3 * 2 + c) * 16 + 16]
                G1 = GT[:, par * 64 + (0 * 2 + c) * 16:par * 64 + (0 * 2 + c) * 16 + 16]
                G2 = GT[:, par * 64 + (1 * 2 + c) * 16:par * 64 + (1 * 2 + c) * 16 + 16]
                dve_stt(A1, mcol(1), 1.0, g1T[:], ALU.add, ALU.mult, ["modT", "g1T"], ["AB"])
                dve_stt(A2, mcol(4), 1.0, g2T[:], ALU.add, ALU.mult, ["modT", "g2T"], ["AB"])
                dve_copy(B1, mcol(0), ["modT"], ["AB"])
                dve_copy(B2, mcol(3), ["modT"], ["AB"])
                dve_copy(G1, mcol(2), ["modT"], [("GT", par)])
                dve_copy(G2, mcol(5), ["modT"], [("GT", par)])

        def ABcol(which, c, e):
            o = (which * 2 + c) * 16 + e
            return AB[:, o:o + 1]

        def GTcol(par, which, c, e):
            o = par * 64 + (which * 2 + c) * 16 + e
            return GT[:, o:o + 1]

        def drain(gen, n=None):
            if gen is None:
                return
            k = 0
            while n is None or k < n:
                try:
                    next(gen)
                except StopIteration:
                    return
                k += 1

        def xs_ap(slot, arena="WS"):
            return ARENA[arena][:, slot * 16 * 512:(slot + 1) * 16 * 512].bitcast(F32).rearrange("p (e c) -> p e c", e=16)

        def norm_phase(mode, which=0):
            drain(norm_gen(mode, which))

        def norm_gen(mode, which=0, xs_arena="WS", intile=False):
            SQk = [("WS", 48), ("WS", 49)]
            SQ = [bfp("WS", 48)[:, 0:256], bfp("WS", 49)[:, 0:256]]
            deep = (mode == "mid" and xs_arena == "WS")
            if deep:
                SQk += [("R2", 40 + i) for i in range(4)]
                SQ += [bfp("R2", 40 + i)[:, 0:256] for i in range(4)]
            NSQ = len(SQ)
            RKs = [[("WS", 50)], [("WS", 51)]]
            Rrs = [f32p("WS", 50, 1), f32p("WS", 51, 1)]
            TMPk = [("R2", 48), ("R2", 49), ("R2", 50), ("R2", 51)]
            TMP = [f32p("R2", 48, 1), f32p("R2", 49, 1), f32p("R2", 50, 1), f32p("R2", 51, 1)]
            if deep:
                TMPk += [("R2", 32 + i) for i in range(4)]
                TMP += [f32p("R2", 32 + i, 1) for i in range(4)]
            NTMP = len(TMP)
            NH2 = 2 * NT

            def stage1(ht):
                t, hh = divmod(ht, 2)
                slot = ht % 3
                XS = xs_ap(slot, xs_arena)
                xk = pg(xs_arena, slot * 16, 16)
                xTk = [("xT", t, e) for e in range(16)]
                RK = RKs[ht % 2]
                Rr = Rrs[ht % 2]
                if mode == "in":
                    for cc in range(2):
                        chunk = 2 * ht + cc
                        rs = (chunk % 6)
                        XT = f32p("R2", rs * 8, 8)
                        dma("sp", XT, xin[chunk * 128:(chunk + 1) * 128, :], [], pg("R2", rs * 8, 8))
                    for e in range(16):
                        b = psa.alloc()
                        for cc in range(2):
                            chunk = 2 * ht + cc
                            rs = chunk % 6
                            XT = f32p("R2", rs * 8, 8)
                            pe_tr(banks[b][:, cc * 128:(cc + 1) * 128], XT[:, e * 128:(e + 1) * 128], id32[:],
                                  pg("R2", rs * 8, 8) + ["id32"], PSK(b))
                        copy_any(e, XS[:, e, :], banks[b][:, 0:256], PSK(b), xk)
                        psa.release(b)
                    dma("sp", xT[t][:, :, hh * 256:(hh + 1) * 256], XS, xk, xTk)
                else:
                    dma("sp", XS, xT[t][:, :, hh * 256:(hh + 1) * 256], xTk, xk)
                sb_ = psa.alloc()
                for e in range(16):
                    act(SQ[e % NSQ], XS[:, e, :], AF.Square, xk, [SQk[e % NSQ]])
                    pe_mm(banks[sb_][:, 0:256], ones16[:], SQ[e % NSQ], (e == 0), (e == 15), [SQk[e % NSQ], "ones16"], PSK(sb_))
                act(Rr, banks[sb_][:, 0:256], AF.Sqrt, PSK(sb_), RK, bias=EPS, scale=1.0 / D)
                psa.release(sb_)
                dve_recip(Rr, Rr, RK, RK)

            def stage2(ht):
                t, hh = divmod(ht, 2)
                c = cond_of(t)
                slot = ht % 3
                XS = xs_ap(slot, xs_arena)
                xk = pg(xs_arena, slot * 16, 16)
                RK = RKs[ht % 2]
                Rr = Rrs[ht % 2]
                if mode != "out":
                    for e in range(16):
                        A = ABcol(2 * which, c, e)
                        B = ABcol(2 * which + 1, c, e)
                        tm = TMP[e % NTMP]
                        dve_stt(tm, XS[:, e, :], A, Rr, ALU.mult, ALU.mult, xk + ["AB"] + RK, [TMPk[e % NTMP]])
                        dst = Hap(t, e)[:, hh * 256:(hh + 1) * 256]
                        if e % 4 == 3:
                            dve_ts(dst, tm, B, None, ALU.add, ALU.bypass, [TMPk[e % NTMP], "AB"], pg("R1", Hpage(t, e)))
                        else:
                            act(dst, tm, AF.Identity, [TMPk[e % NTMP], "AB"], pg("R1", Hpage(t, e)), bias=B, scale=1.0)
                else:
                    for e in range(16):
                        dve_stt(XS[:, e, :], XS[:, e, :], gfT[:, e:e + 1], Rr, ALU.mult, ALU.mult, xk + ["gfT"] + RK, xk)
                    for cc in range(2):
                        chunk = 2 * ht + cc
                        rs = chunk % 6
                        OT = f32p("R2", rs * 8, 8)
                        ok = pg("R2", rs * 8, 8)
                        for e4 in range(4):
                            b = psa.alloc()
                            for q in range(4):
                                e = e4 * 4 + q
                                pe_tr(banks[b][:, q * 128:(q + 1) * 128], XS[:, e, cc * 128:(cc + 1) * 128], id32[:], xk + ["id32"], PSK(b))
                            copy_any(e4, OT[:, e4 * 512:(e4 + 1) * 512], banks[b][:], PSK(b), ok)
                            psa.release(b)
                        dma("sp", yout[chunk * 128:(chunk + 1) * 128, :], OT, ok, [("yout", chunk)])

            if intile:
                for t_ in range(NT):
                    stage1(2 * t_)
                    stage1(2 * t_ + 1)
                    stage2(2 * t_)
                    stage2(2 * t_ + 1)
                    yield
                return
            stage1(0)
            for ht in range(NH2):
                if ht + 1 < NH2:
                    stage1(ht + 1)
                stage2(ht)
                if ht % 2 == 1:
                    yield

        def qk_norm(pb, t, gcol, st, out_bf, out_keys, i, out32=None, out32_keys=None):
            rope = t >= 2
            sqk, sq = st["sq"][i % 2]
            rk, rr = st["r"][i % 2]
            act(sq, banks[pb][:], AF.Square, PSK(pb), sqk)
            sb_ = psa.alloc()
            pe_mm(banks[sb_][:], ones16[:], sq, True, True, sqk + ["ones16"], PSK(sb_))
            act(rr, banks[sb_][:], AF.Sqrt, PSK(sb_), rk, bias=EPS, scale=1.0 / HD)
            psa.release(sb_)
            dve_recip(rr, rr, rk, rk)
            if not rope:
                if out32 is None:
                    dve_stt(out_bf, banks[pb][:], gcol, rr, ALU.mult, ALU.mult, PSK(pb) + ["qkg"] + rk, out_keys)
                else:
                    dve_stt(out32, banks[pb][:], gcol, rr, ALU.mult, ALU.mult, PSK(pb) + ["qkg"] + rk, out32_keys)
                    act(out_bf, out32, AF.Copy, out32_keys, out_keys)
                psa.release(pb)
                return
            q32k, q32 = st["qn"][i % 2]
            qnk, qn = st["qn16"][i % 2]
            t1k, t1 = st["t1"][0]
            ck_, cs_ = get_rope(t)
            dve_stt(qn, banks[pb][:], gcol, rr, ALU.mult, ALU.mult, PSK(pb) + ["qkg"] + rk, qnk)
            psa.release(pb)
            rb = psa.alloc()
            pe_mm(banks[rb][:], rm16[:], qn, True, True, qnk + ["rm16"], PSK(rb))
            dve_tt(t1, banks[rb][:], cs_[1], ALU.mult, PSK(rb) + ck_, t1k)
            psa.release(rb)
            pool_tt(q32, qn, cs_[0], ALU.mult, qnk + ck_, q32k)
            pool_tt(out_bf, q32, t1, ALU.add, q32k + t1k, out_keys)

        def qk_pipeline(units, st):
            n = len(units)
            S = [None] * n

            def stageA(u):
                un = units[u]
                pb = psa.alloc()
                S[u] = pb
                proj_group(un["w3"], un["wk"], un["t"], pb)
                sqk, sq = st["sq"][u % 3]
                act(sq, banks[pb][:], AF.Square, PSK(pb), sqk)

            def stageB(u):
                un = units[u]
                pb = S[u]
                t = un["t"]
                sqk, sq = st["sq"][u % 3]
                rk, rr = st["r"][u % 3]
                sb_ = psa.alloc()
                pe_mm(banks[sb_][:], ones16[:], sq, True, True, sqk + ["ones16"], PSK(sb_))
                act(rr, banks[sb_][:], AF.Sqrt, PSK(sb_), rk, bias=EPS, scale=1.0 / HD)
                psa.release(sb_)
                dve_recip(rr, rr, rk, rk)
                if t >= 2:
                    qnk, qn = st["qn16"][u % 3]
                    dve_stt(qn, banks[pb][:], un["gcol"], rr, ALU.mult, ALU.mult, PSK(pb) + ["qkg"] + rk, qnk)
                elif un["nk"] is None:
                    dve_stt(un["out_bf"], banks[pb][:], un["gcol"], rr, ALU.mult, ALU.mult, PSK(pb) + ["qkg"] + rk, un["out_keys"])
                else:
                    o32k, o32 = st["q32"][u % 2]
                    dve_stt(o32, banks[pb][:], un["gcol"], rr, ALU.mult, ALU.mult, PSK(pb) + ["qkg"] + rk, o32k)
                    act(un["out_bf"], o32, AF.Copy, o32k, un["out_keys"])
                psa.release(pb)

            def stageC(u):
                un = units[u]
                t = un["t"]
                if t >= 2:
                    qnk, qn = st["qn16"][u % 3]
                    t1k, t1 = st["t1"][u % 2]
                    q32k, q32 = st["q32"][u % 2]
                    ck_, cs_ = get_rope(t)
                    rb = psa.alloc()
                    pe_mm(banks[rb][:], rm16[:], qn, True, True, qnk + ["rm16"], PSK(rb))
                    act(t1, banks[rb][:], AF.Copy, PSK(rb), t1k)
                    psa.release(rb)
                    pool_tt(t1, t1, cs_[1], ALU.mult, t1k + ck_, t1k)
                    pool_tt(q32, qn, cs_[0], ALU.mult, qnk + ck_, q32k)
                    pool_tt(un["out_bf"], q32, t1, ALU.add, q32k + t1k, un["out_keys"])
                elif un["nk"] is not None:
                    l_, kvh = un["nk"]
                    o32k, o32 = st["q32"][u % 2]
                    tb = psa.alloc()
                    for c4 in range(4):
                        pe_tr(banks[tb][:, c4 * 128:(c4 + 1) * 128], o32[:, c4 * 128:(c4 + 1) * 128], id32[:], o32k + ["id32"], PSK(tb))
                    nkk, nks = st["t1"][1]
                    copy_any(1, nks, banks[tb][:], PSK(tb), nkk)
                    psa.release(tb)
                    for s2 in range(2):
                        dst = nk_d[2 * t + s2, l_, :, kvh * 128:(kvh + 1) * 128].rearrange("(h p) d -> p h d", p=128)
                        dma("sp", dst, nks[:, s2 * 256:(s2 + 1) * 256].rearrange("p (h d) -> p h d", h=2), nkk, [("nk", l_, t, kvh, s2)])

            for i in range(n + 2):
                if i < n:
                    stageA(i)
                if 0 <= i - 1 < n:
                    stageB(i - 1)
                if 0 <= i - 2 < n:
                    stageC(i - 2)

        rope_u = {"n": 0}

        def get_rope(t):
            sl = rope_u["n"] % 2
            rope_u["n"] += 1
            p = 44 + 4 * sl
            Cc = f32p("WS", p, 2)
            Ss = f32p("WS", p + 2, 2)
            dma("sp", Cc, ropeC_d[:, (t - 2) * 512:(t - 1) * 512], [], pg("WS", p, 2))
            dma("sp", Ss, ropeS_d[:, (t - 2) * 512:(t - 1) * 512], [], pg("WS", p + 2, 2))
            return pg("WS", p, 4), (Cc, Ss)

        def KT(kvh, tok0, n):
            o = kvh * KTOK + tok0
            return R2[:, o:o + n]

        def KTkeys(kvh, tok0, n):
            o = kvh * KTOK + tok0
            return pg("R2", o // 512, (o + n - 1) // 512 - o // 512 + 1)

        def Vap(c, kvh):
            return R2[:, (26 + c) * 512 + kvh * 128:(26 + c) * 512 + (kvh + 1) * 128]

        def tile_tok0(t):
            return t * 512

        def proj_group(w3, wk, t, pb):
            pairs = [(w3[:, kc, :], Hap(t, kc)) for kc in range(16)]
            mm_group(banks[pb][:], pairs, wk + pg("R1", t * 16, 16), PSK(pb))

        def phase2(l):
            wl = w_in_d[l]
            wstate["limit"] = 18
            wstate["pos"] = 0
            st = {
                "sq": [(pg("WS", 24 + i), bfp("WS", 24 + i)) for i in range(3)],
                "qn16": [(pg("WS", 27 + i), bfp("WS", 27 + i)) for i in range(3)],
                "r": [(pg("WS", 30 + 2 * i, 2), f32p("WS", 30 + 2 * i)) for i in range(3)],
                "q32": [(pg("WS", 36 + 2 * i, 2), f32p("WS", 36 + 2 * i)) for i in range(2)],
                "qn": [(pg("WS", 36 + 2 * i, 2), f32p("WS", 36 + 2 * i)) for i in range(2)],
                "t1": [(pg("WS", 40 + 2 * i, 2), f32p("WS", 40 + 2 * i)) for i in range(2)],
                "o": [(pg("WS", 42), bfp("WS", 42)), (pg("WS", 43), bfp("WS", 43))],
            }
            ckp = walloc(2)
            CK16 = WS[:, ckp * 512:(ckp + 2) * 512].rearrange("p (h c) -> p h c", h=2)
            dma("pool", CK16, ck_d[l].rearrange("(h p) c -> p h c", p=128), [], pg("WS", ckp, 2))
            Vc = R2[:, (26 + 24) * 512:(26 + 26) * 512].rearrange("p (h c) -> p h c", h=2)
            dma("pool", Vc, cv_d[l].rearrange("(h p) c -> p h c", p=128), [], pg("R2", 50, 2))
            b = psa.alloc()
            b16 = banks[b][:].bitcast(BF16)
            for kvh in range(KVH):
                for h2 in range(2):
                    o = (kvh * 2 + h2) * 128
                    pe_tr(b16[:, o:o + 128], CK16[:, h2, kvh * 128:(kvh + 1) * 128], id16[:], pg("WS", ckp, 2) + ["id16"], PSK(b))
            for kvh in range(KVH):
                dve_copy(KT(kvh, NTOK, PAST), b16[:, kvh * 256:(kvh + 1) * 256], PSK(b), KTkeys(kvh, NTOK, PAST))
            psa.release(b)
            if DBG.get("ph2", 99) < 1:
                return
            kg = qkg[:, 2 * l + 1:2 * l + 2]
            qg = qkg[:, 2 * l:2 * l + 1]
            units = []
            for kvh in range(KVH):
                w3, wk = wload(wl[:, C_K + kvh * 128:C_K + (kvh + 1) * 128], 16, 128)
                for t in range(NT):
                    units.append(dict(w3=w3, wk=wk, t=t, gcol=kg, out_bf=KT(kvh, t * 512, 512), out_keys=KTkeys(kvh, t * 512, 512),
                                      nk=(l, kvh) if t < 2 else None))
            qk_pipeline(units, st)
            if DBG.get("ph2", 99) < 2:
                return
            w3, wk = wload(wl[:, C_V:C_V + 512], 16, 512)
            vdbg = DBG.get("v", 99)
            for c in range(24):
                if vdbg < 1:
                    break
                t, c4 = divmod(c, 4)
                pb = psa.alloc()
                pairs = [(Hap(t, kc)[:, c4 * 128:(c4 + 1) * 128], w3[:, kc, :]) for kc in range(16)]
                mm_group(banks[pb][:], pairs, wk + pg("R1", t * 16, 16), PSK(pb))
                copy_any(c, bfp("R2", 26 + c), banks[pb][:], PSK(pb), pg("R2", 26 + c))
                if t < 2 and vdbg >= 2:
                    vk, v32 = st["qn"][c % 2]
                    copy_any(c + 1, v32, banks[pb][:], PSK(pb), vk)
                    seq, pos0 = divmod(c * 128, PSEQ)
                    if vdbg >= 3:
                        dma("sp", nv_d[seq, l, pos0:pos0 + 128, :], v32, vk, [("nv", l, c)])
                psa.release(pb)
            if DBG.get("ph2", 99) < 3:
                return
            scale = 1.0 / math.sqrt(HD)
            for h in range(NH):
                g = h // 4
                w3, wk = wload(wl[:, C_Q + h * 128:C_Q + (h + 1) * 128], 16, 128)
                units = [dict(w3=w3, wk=wk, t=t, gcol=qg, out_bf=bfp("WS", 18 + t), out_keys=pg("WS", 18 + t), nk=None) for t in range(NT)]
                qk_pipeline(units, st)
                attention_head(l, h, g, st, scale)

        def attn_evacuate(h, t, ob, db, st):
            o32k, o32 = st["qn"][t % 2]
            dk, d32 = st["r"][t % 2]
            dve_copy(o32, banks[ob][:], PSK(ob), o32k)
            psa.release(ob)
            act(d32, banks[db][:], AF.Copy, PSK(db), dk)
            psa.release(db)
            dve_recip(d32, d32, dk, dk)
            osk, osb = st["o"][t % 2]
            dve_tt(osb, o32, d32, ALU.mult, o32k + dk, osk)
            dma("sp", Osp[t][:, h, :], osb, osk, [("O", t, h)])

        def pe_batch(items):
            specs = [it[0] for it in items]
            rd, wr = [], []
            for it in items:
                rd += it[1]
                wr += it[2]
            pe_multi(specs, rd, wr)

        def attention_head(l, h, g, st, scale):
            PTk = [pg("WS", 24 + i) for i in range(6)]
            PT = [bfp("WS", 24 + i) for i in range(6)]
            for t in range(2):
                Q = bfp("WS", 18 + t)
                Qkeys = pg("WS", 18 + t)
                ob = psa.alloc()
                db = psa.alloc()
                sbs = {}
                for j in range(2):
                    s_ = psa.alloc()
                    sbs[j] = s_
                    specs, rk = [], []
                    for a_ in range(2):
                        tok0 = (2 * t + a_) * PSEQ + j * 128
                        specs.append((banks[s_][:, a_ * 256:(a_ + 1) * 256], KT(g, tok0, 128), Q[:, a_ * 256:(a_ + 1) * 256], (a_ == 0), True))
                        rk += KTkeys(g, tok0, 128)
                    pe_multi(specs, rk + Qkeys, PSK(s_))
                for j in range(2):
                    s_ = sbs.pop(j)
                    pt, ptk = PT[j], PTk[j]
                    act(pt, banks[s_][:], AF.Exp, PSK(s_), ptk, scale=scale)
                    psa.release(s_)
                    specs, rk = [], []
                    for a_ in range(2):
                        vc = (2 * t + a_) * 2 + j
                        specs.append((banks[ob][:, a_ * 256:(a_ + 1) * 256], Vap(vc, g), pt[:, a_ * 256:(a_ + 1) * 256], (j == 0 and a_ == 0), (j == 1)))
                        rk += pg("R2", 26 + vc)
                    specs.append((banks[db][:], ones16[:], pt, (j == 0), (j == 1)))
                    pe_multi(specs, rk + ptk + ["ones16"], PSK(ob) + PSK(db))
                attn_evacuate(h, t, ob, db, st)
            NP = 9
            stream = [(t, p) for t in range(2, NT) for p in range(NP)]
            sbs = {}

            def S_item(t, j):
                s_ = psa.alloc()
                sbs[(t, j)] = s_
                tok0 = 1024 + j * 128
                return ((banks[s_][:], KT(g, tok0, 128), bfp("WS", 18 + t), True, True), KTkeys(g, tok0, 128) + pg("WS", 18 + t), PSK(s_))
            pe_batch([S_item(t, j) for (t, p) in stream[:2] for j in (2 * p, 2 * p + 1)])
            obdb = {}
            for idx, (t, p) in enumerate(stream):
                if t not in obdb:
                    obdb[t] = (psa.alloc(), psa.alloc())
                if p == NP - 3 and t + 1 < NT:
                    obdb[t + 1] = (psa.alloc(), psa.alloc())
                ob, db = obdb[t]
                items = []
                for jj, j in enumerate((2 * p, 2 * p + 1)):
                    s_ = sbs.pop((t, j))
                    k = (2 * idx + jj) % 6
                    pt, ptk = PT[k], PTk[k]
                    act(pt, banks[s_][:], AF.Exp, PSK(s_), ptk, scale=scale)
                    psa.release(s_)
                    vc = 8 + j
                    items.append(((banks[ob][:], Vap(vc, g), pt, (j == 0), (j == 17)), pg("R2", 26 + vc) + ptk, PSK(ob)))
                    items.append(((banks[db][:], ones16[:], pt, (j == 0), (j == 17)), ptk + ["ones16"], PSK(db)))
                if idx + 2 < len(stream):
                    t2, p2 = stream[idx + 2]
                    pe_batch([S_item(t2, 2 * p2), S_item(t2, 2 * p2 + 1)])
                pe_batch(items)
                if p == NP - 1:
                    attn_evacuate(h, t, ob, db, st)

        def phase2_conv_gates(l):
            wl = w_in_d[l]
            wstate["limit"] = WRING
            wstate["pos"] = 0
            CB = [(pg("WS", 24 + 2 * i, 2), f32p("WS", 24 + 2 * i)) for i in range(3)]
            PR = [(pg("WS", 30 + 2 * i, 2), f32p("WS", 30 + 2 * i)) for i in range(3)]
            CS = [(pg("WS", 36 + 2 * i, 2), f32p("WS", 36 + 2 * i)) for i in range(2)]
            ACk, AC = pg("WS", 40, 2), f32p("WS", 40)
            u = 0
            for j in range(8):
                def cw(k, j=j):
                    return convT[:, j * 4 + k:j * 4 + k + 1]
                wpage = walloc(12)
                wb3, wbk = wload(wl[:, C_CB + j * 128:C_CB + (j + 1) * 128], 16, 128, page=wpage)
                wc3, wck = wload(wl[:, C_CC + j * 128:C_CC + (j + 1) * 128], 16, 128, page=wpage + 4)
                wx3, wxk = wload(wl[:, C_CX + j * 128:C_CX + (j + 1) * 128], 16, 128, page=wpage + 8)

                def conv_tile(t, j=j, cw=cw):
                    cbk, cb = CB[t % 3]
                    prk, pr = PR[t % 3]
                    dve_ts(AC, pr, cw(1), cw(3), ALU.mult, ALU.add, prk + ["convT"], ACk)
                    segs = [(0, 256), (256, 512)] if t < 2 else [(0, 512)]
                    for (a, b_) in segs:
                        dve_stt(AC[:, a + 1:b_], pr[:, a:b_ - 1], cw(0), AC[:, a + 1:b_], ALU.mult, ALU.add, prk + ACk + ["convT"], ACk)
                        dve_stt(AC[:, a:b_ - 1], pr[:, a + 1:b_], cw(2), AC[:, a:b_ - 1], ALU.mult, ALU.add, prk + ACk + ["convT"], ACk)
                    if t > 2:
                        pk2, pr2 = PR[(t - 1) % 3]
                        dve_stt(AC[:, 0:1], pr2[:, 511:512], cw(0), AC[:, 0:1], ALU.mult, ALU.add, pk2 + ACk + ["convT"], ACk)
                    if 2 <= t < NT - 1:
                        pk2, pr2 = PR[(t + 1) % 3]
                        dve_stt(AC[:, 511:512], pr2[:, 0:1], cw(2), AC[:, 511:512], ALU.mult, ALU.add, pk2 + ACk + ["convT"], ACk)
                    dve_tt(bfp("R2", j * 6 + t), cb, AC, ALU.mult, cbk + ACk, pg("R2", j * 6 + t))

                for t in range(NT):
                    cbk, cb = CB[t % 3]
                    prk, pr = PR[t % 3]
                    csk, cs = CS[u % 2]
                    pb = psa.alloc()
                    proj_group(wb3, wbk, t, pb)
                    act(cb, banks[pb][:], AF.Copy, PSK(pb), cbk)
                    psa.release(pb)
                    pb = psa.alloc()
                    proj_group(wc3, wck, t, pb)
                    act(cs, banks[pb][:], AF.Copy, PSK(pb), csk)
                    psa.release(pb)
                    pb = psa.alloc()
                    proj_group(wx3, wxk, t, pb)
                    dve_tt(pr, banks[pb][:], cs, ALU.mult, PSK(pb) + csk, prk)
                    psa.release(pb)
                    u += 1
                    if t >= 1:
                        conv_tile(t - 1)
                conv_tile(NT - 1)
            SGk = [pg("WS", 24 + i) for i in range(4)]
            SG = [bfp("WS", 24 + i) for i in range(4)]
            u = 0
            for e2 in range(32):
                which, e = divmod(e2, 16)
                w3, wk = wload(wl[:, C_GA + e2 * 128:C_GA + (e2 + 1) * 128], 16, 128)
                for t in range(NT):
                    pb = psa.alloc()
                    proj_group(w3, wk, t, pb)
                    act(SG[u % 4], banks[pb][:], AF.Sigmoid, PSK(pb), SGk[u % 4])
                    psa.release(pb)
                    dma("sp", Gsp[t][:, 2 * e + which, :], SG[u % 4], SGk[u % 4], [("G", t, 2 * e + which)])
                    u += 1

        def load_R1(src, nchunk, t_list, keyname):
            for t in t_list:
                tt = t_list.index(t)
                nparts = 4 if nchunk > 16 else 2
                for half in range(nparts):
                    n2 = nchunk // nparts
                    c0 = half * n2
                    dst = R1[:, (tt * nchunk + c0) * 512:(tt * nchunk + c0 + n2) * 512].rearrange("p (c k) -> p c k", c=n2)
                    dma("sp", dst, src[t][:, c0:c0 + n2, :], [(keyname, t, c) for c in range(c0, c0 + n2)], pg("R1", tt * nchunk + c0, n2))

        def phase3(l):
            load_R1(Osp, 16, list(range(NT)), "O")
            GLk = [pg("WS", 24 + 2 * i, 2) for i in range(3)]
            GL = [bfp("WS", 24 + 2 * i, 2).rearrange("p (w c) -> p w c", w=2) for i in range(3)]
            T1 = [(pg("WS", 30 + 2 * i, 2), f32p("WS", 30 + 2 * i)) for i in range(2)]
            T2 = [(pg("WS", 34 + 2 * i, 2), f32p("WS", 34 + 2 * i)) for i in range(2)]
            MOk = [pg("WS", 38 + i) for i in range(3)]
            MO = [bfp("WS", 38 + i) for i in range(3)]
            units = [(e, t) for e in range(16) for t in range(NT)]
            PF = 2

            def gload(i):
                e, t = units[i]
                dma("sp", GL[i % 3], Gsp[t][:, 2 * e:2 * e + 2, :], [("G", t, 2 * e), ("G", t, 2 * e + 1)], GLk[i % 3])
            for i in range(PF):
                gload(i)
            wa3 = wb3 = None
            for i, (e, t) in enumerate(units):
                if t == 0:
                    wp = walloc(6)
                    wa3, wak = wload(w_a_d[l][:, e * 128:(e + 1) * 128], 8, 128, page=wp)
                    wb3, wbk = wload(w_b_d[l][:, e * 128:(e + 1) * 128], 16, 128, page=wp + 2)
                if i + PF < len(units):
                    gload(i + PF)
                ua = psa.alloc()
                pairs = [(wa3[:, kc, :], bfp("R2", kc * 6 + t)) for kc in range(8)]
                mm_group(banks[ua][:], pairs, wak + [("R2", kc * 6 + t) for kc in range(8)], PSK(ua))
                ub = psa.alloc()
                pairs = [(wb3[:, kc, :], Hap(t, kc)) for kc in range(16)]
                mm_group(banks[ub][:], pairs, wbk + pg("R1", t * 16, 16), PSK(ub))
                t1k, t1 = T1[i % 2]
                t2k, t2 = T2[i % 2]
                gl = GL[i % 3]
                dve_tt(t1, banks[ua][:], gl[:, 0, :], ALU.mult, PSK(ua) + GLk[i % 3], t1k)
                psa.release(ua)
                dve_tt(t2, banks[ub][:], gl[:, 1, :], ALU.mult, PSK(ub) + GLk[i % 3], t2k)
                psa.release(ub)
                mo = MO[i % 3]
                dve_tt(mo, t1, t2, ALU.add, t1k + t2k, MOk[i % 3])
                dma("sp", Msp[t][:, e, :], mo, MOk[i % 3], [("M", t, e)])

        def resid_phase(l, wsrc, kc_n, which_gt, t_list, rslot, e_list=None, tile_major=False, after_tile=None):
            XL = [(pg("WS", 24 + 2 * i, 2), f32p("WS", 24 + 2 * i)) for i in range(4)]
            if e_list is None:
                e_list = list(range(16))
            if tile_major:
                units = [(e, t) for t in t_list for e in e_list]
            else:
                units = [(e, t) for e in e_list for t in t_list]
            PF = 2

            def xload(i):
                e, t = units[i]
                dma("sp", XL[i % 4][1], xT[t][:, e, :], [("xT", t, e)], XL[i % 4][0])
            for i in range(min(PF, len(units))):
                xload(i)
            wcache = {}
            for i, (e, t) in enumerate(units):
                if e not in wcache:
                    wcache[e] = wload(wsrc[:, e * 128:(e + 1) * 128], kc_n, 128)
                w3, wk = wcache[e]
                if i + PF < len(units):
                    xload(i + PF)
                tt = t_list.index(t)
                pb = psa.alloc()
                nsub = 4 if kc_n > 16 else 1
                step = kc_n // nsub
                for sg in range(nsub):
                    k0, k1 = sg * step, (sg + 1) * step
                    pairs = [(w3[:, kc, :], bfp("R1", tt * kc_n + kc)) for kc in range(k0, k1)]
                    mm_group(banks[pb][:], pairs, wk + pg("R1", tt * kc_n + k0, k1 - k0), PSK(pb), first=(sg == 0), last=(sg == nsub - 1))
                xk, xs = XL[i % 4]
                gcol = GTcol(l % 2, which_gt, cond_of(t), e)
                dve_stt(xs, banks[pb][:], gcol, xs, ALU.mult, ALU.add, PSK(pb) + xk + [("GT", l % 2)], xk)
                psa.release(pb)
                dma("sp", xT[t][:, e, :], xs, xk, [("xT", t, e)])
                if tile_major and after_tile is not None and e == e_list[-1]:
                    after_tile(t)

        def phase4(l):
            load_R1(Msp, 16, list(range(NT)), "M")
            tl = list(range(NT))
            if not DBG.get("ovl4", 1):
                resid_phase(l, w_o_d[l], 16, 0, tl, 0)
                norm_phase("mid", which=1)
                return
            NTAIL = 4
            resid_phase(l, w_o_d[l], 16, 0, tl, 0, e_list=list(range(16 - NTAIL)))
            wstate["pos"] = 0
            ng = norm_gen("mid", 1, "R2", intile=True)
            resid_phase(l, w_o_d[l], 16, 0, tl, 0, e_list=list(range(16 - NTAIL, 16)), tile_major=True, after_tile=lambda t: next(ng))
            drain(ng)

        def phase6(l, bg=None, normgen=None):
            SIk = [pg("WS", 24 + 2 * i, 2) for i in range(3)]
            SI = [f32p("WS", 24 + 2 * i) for i in range(3)]
            AOk = [pg("WS", 30 + i) for i in range(4)]
            AO = [bfp("WS", 30 + i) for i in range(4)]
            ust = {"u": 0}
            blocks = {}

            def load(f):
                wp = walloc(8)
                wg3, wgk = wload(w_gate_d[l][:, f * 128:(f + 1) * 128], 16, 128, page=wp)
                wu3, wuk = wload(w_up_d[l][:, f * 128:(f + 1) * 128], 16, 128, page=wp + 4)
                blocks[f] = (wg3, wgk, wu3, wuk)

            def unit(f, t):
                wg3, wgk, wu3, wuk = blocks[f]
                u = ust["u"]
                gb = psa.alloc()
                proj_group(wg3, wgk, t, gb)
                ub = psa.alloc()
                proj_group(wu3, wuk, t, ub)
                si = SI[u % 3]
                act(si, banks[gb][:], AF.Silu, PSK(gb), SIk[u % 3])
                psa.release(gb)
                ao = AO[u % 4]
                dve_tt(ao, banks[ub][:], si, ALU.mult, PSK(ub) + SIk[u % 3], AOk[u % 4])
                psa.release(ub)
                dma("sp", Asp[t][:, f, :], ao, AOk[u % 4], [("A", t, f)])
                ust["u"] = u + 1

            f0 = 0
            if normgen is not None:
                wstate["pos"] = 0
                f0 = 3
                for f in range(f0):
                    load(f)
                for t in range(NT):
                    next(normgen)
                    for f in range(f0):
                        unit(f, t)
                drain(normgen)
            for f in range(f0, NF):
                load(f)
                for t in range(NT):
                    unit(f, t)
                drain(bg, 3)
            drain(bg)

        phases = []
        for l in range(depth):
            if l == 0:
                phases.append(lambda: drain(adaln_gen(0)))
            phases.append(lambda l=l: norm_phase("in" if l == 0 else "mid", which=0))
            phases.append(lambda l=l: phase2(l))
            phases.append(lambda l=l: phase2_conv_gates(l))
            phases.append(lambda l=l: phase3(l))
            phases.append(lambda l=l: phase4(l))
            phases.append(lambda l=l: phase6(l, adaln_gen(l + 1) if l + 1 < depth else None))

            def ph7(l=l):
                for third in range(3):
                    tl = [2 * third, 2 * third + 1]
                    load_R1(Asp, NF, tl, "A")
                    resid_phase(l, w_down_d[l], NF, 1, tl, 0)
            phases.append(ph7)
        phases.append(lambda: norm_phase("out"))
        for i, ph in enumerate(phases):
            if maxph is not None and i >= maxph and i != len(phases) - 1:
                continue
            ph()

        emit(nc, P, sems)
    return nc


def _rope_tables():
    rows = STOK // 64
    row = np.repeat(np.arange(rows, dtype=np.float32), 64)
    col = np.tile(np.arange(64, dtype=np.float32), rows)
    n_freq = HD // 4
    inv = (np.float32(10000.0) ** (-np.arange(n_freq, dtype=np.float32) / np.float32(n_freq))).astype(np.float32)
    ang = np.concatenate([row[:, None] * inv, col[:, None] * inv], axis=-1).astype(np.float32)
    cos = np.cos(ang).astype(np.float32)
    sin = np.sin(ang).astype(np.float32)
    C = np.ascontiguousarray(np.concatenate([cos, cos], axis=1).T)
    S = np.ascontiguousarray(np.concatenate([sin, sin], axis=1).T)
    return C, S


def _rot_matrix_T():
    Rm = np.zeros((128, 128), np.float32)
    for d in range(64):
        Rm[d, d + 64] = -1.0
        Rm[d + 64, d] = 1.0
    return np.ascontiguousarray(Rm.T)


def make_in_maps(inp, depth=DEPTH):
    f = lambda a: np.ascontiguousarray(np.asarray(a, dtype=np.float32))
    x_prompt, x_sample = f(inp["x_prompt"]), f(inp["x_sample"])
    cache_k, cache_v = f(inp["cache_k"]), f(inp["cache_v"])
    c, c_ctx = f(inp["c"]), f(inp["c_ctx"])
    C, S = _rope_tables()
    shared = {
        "w_ada": f(inp["w_ada"][:depth]), "w_in": f(inp["w_in"][:depth]), "w_a": f(inp["w_a"][:depth]), "w_b": f(inp["w_b"][:depth]),
        "w_o": f(inp["w_o"][:depth]), "w_gate": f(inp["w_gate"][:depth]), "w_up": f(inp["w_up"][:depth]), "w_down": f(inp["w_down"][:depth]),
        "badaT": np.ascontiguousarray(f(inp["b_ada"]).reshape(DEPTH, 96, 128).transpose(0, 2, 1)),
        "g1T": np.ascontiguousarray(f(inp["norm1"]).reshape(DEPTH, 16, 128).transpose(0, 2, 1)),
        "g2T": np.ascontiguousarray(f(inp["norm2"]).reshape(DEPTH, 16, 128).transpose(0, 2, 1)),
        "gfT": np.ascontiguousarray(f(inp["norm_f"]).reshape(16, 128).T),
        "ropeC": C, "ropeS": S,
        "ident": np.eye(128, dtype=np.float32),
        "rmT": _rot_matrix_T(),
    }
    qkg = np.zeros((128, 2 * DEPTH), np.float32)
    for l in range(DEPTH):
        qkg[:, 2 * l] = inp["q_gain"][l]
        qkg[:, 2 * l + 1] = inp["k_gain"][l]
    shared["qkg"] = qkg
    cw = f(inp["conv_w"]).reshape(DEPTH, 3, 8, 128)
    cb = f(inp["conv_b"]).reshape(DEPTH, 8, 128)
    convT = np.zeros((DEPTH, 128, 8, 4), np.float32)
    convT[:, :, :, 0:3] = cw.transpose(0, 3, 2, 1)
    convT[:, :, :, 3] = cb.transpose(0, 2, 1)
    shared["convT"] = np.ascontiguousarray(convT.reshape(DEPTH, 128, 32))
    maps = []
    for i in range(NCORE):
        m = dict(shared)
        m["xin"] = np.ascontiguousarray(np.concatenate(
            [x_prompt[NPSEQ * i:NPSEQ * (i + 1)].reshape(NPSEQ * PSEQ, D), x_sample[i]], axis=0))
        m["ck"] = np.ascontiguousarray(cache_k[i].reshape(DEPTH, PAST, 512))
        m["cv"] = np.ascontiguousarray(cache_v[i].reshape(DEPTH, PAST, 512))
        cond = np.stack([c_ctx, c[i]], axis=0)
        m["condT"] = np.ascontiguousarray(cond.reshape(2, 16, 128).transpose(2, 1, 0).reshape(128, 32))
        maps.append(m)
    return maps


_NC_CACHE = {}


def kernel(**inputs):
    if "nc" not in _NC_CACHE:
        _NC_CACHE["nc"] = build(DEPTH)
    nc = _NC_CACHE["nc"]
    maps = make_in_maps(inputs)
    res = run_bass_kernel_spmd(nc, maps, core_ids=list(range(NCORE)))
    B = NCORE * NPSEQ
    y_prompt = np.empty((B, PSEQ, D), np.float32)
    y_sample = np.empty((NCORE, STOK, D), np.float32)
    nk = np.empty((B, DEPTH, PSEQ, KVH, HD), np.float32)
    nv = np.empty((B, DEPTH, PSEQ, KVH, HD), np.float32)
    for i in range(NCORE):
        r = res.results[i]
        y = r["yout"]
        y_prompt[NPSEQ * i:NPSEQ * (i + 1)] = y[:NPSEQ * PSEQ].reshape(NPSEQ, PSEQ, D)
        y_sample[i] = y[NPSEQ * PSEQ:]
        nk[NPSEQ * i:NPSEQ * (i + 1)] = r["nk"].reshape(NPSEQ, DEPTH, PSEQ, KVH, HD)
        nv[NPSEQ * i:NPSEQ * (i + 1)] = r["nv"].reshape(NPSEQ, DEPTH, PSEQ, KVH, HD)
    return (y_prompt, y_sample, nk, nv)
```
